# Optimizing a Trainium2 kernel written in Bass

```python
import jax, jax.numpy as jnp
from jax import lax
import numpy as np

D_MODEL = 1024
BATCH = 16
SEQ = 2048
DEPTH = 2

N_EVEN = (DEPTH + 1) // 2
N_ODD = DEPTH // 2
EPS = 1e-6
CONV_WIDTH = 4

LRU_WIDTH = D_MODEL // 2
LRU_BLOCKS = 8
LRU_BLOCK = LRU_WIDTH // LRU_BLOCKS
LRU_C = 8.0
RET_HEADS = 4
RET_DK = 128
RET_DV = 128
RET_CHUNK = 64
ROPE_BASE = 10000.0
HG_HEADS = 4
HG_DK = 128
HG_DV = 128
HG_CHUNK = 32
GD_HEADS = 4
GD_DK = 128
GD_DV = 128
GD_CHUNK = 64
MLP_HIDDEN = 4 * D_MODEL

EVEN_SIZES = (LRU_WIDTH, LRU_WIDTH, RET_HEADS * RET_DK, RET_HEADS * RET_DK, RET_HEADS * RET_DV, RET_HEADS * RET_DV)
EVEN_IN = sum(EVEN_SIZES)
EVEN_MIX = LRU_WIDTH + RET_HEADS * RET_DV
ODD_SIZES = (HG_HEADS * HG_DK, HG_HEADS * HG_DK, HG_HEADS * HG_DV, HG_HEADS * HG_DV,
             GD_HEADS * GD_DK, GD_HEADS * GD_DK, GD_HEADS * GD_DV, GD_HEADS * GD_DV, GD_HEADS, GD_HEADS)
ODD_IN = sum(ODD_SIZES)
ODD_MIX = HG_HEADS * HG_DV + GD_HEADS * GD_DV

kernel_name = 'hybrid_rglru_retention_hgrn2_gdn_adaln'


def _split(t, sizes):
    return jnp.split(t, np.cumsum(sizes)[:-1].tolist(), axis=-1)


def rmsnorm(x, w):
    xf = x.astype(jnp.float32)
    y = xf * lax.rsqrt(jnp.mean(xf * xf, axis=-1, keepdims=True) + EPS)
    return (y * w.astype(jnp.float32)).astype(x.dtype)


def modulate(h, shift, scale):
    return h * (1.0 + scale[:, None, :]) + shift[:, None, :]


def causal_depthwise_conv(x, w, b=None):
    C = x.shape[-1]
    y = lax.conv_general_dilated(x, w[:, None, :].astype(x.dtype), window_strides=(1,),
                                 padding=[(w.shape[0] - 1, 0)],
                                 dimension_numbers=('NWC', 'WIO', 'NWC'), feature_group_count=C)
    if b is not None:
        y = y + b.astype(x.dtype)
    return y


def rotary(x, positions):
    half = x.shape[-1] // 2
    inv = ROPE_BASE ** (-jnp.arange(half, dtype=jnp.float32) / half)
    ang = positions.astype(jnp.float32)[..., None] * inv
    cos = jnp.cos(ang)[:, :, None, :]
    sin = jnp.sin(ang)[:, :, None, :]
    x1, x2 = x[..., :half], x[..., half:]
    return jnp.concatenate([x1 * cos - x2 * sin, x1 * sin + x2 * cos], axis=-1)


def l2norm(x):
    return x * lax.rsqrt(jnp.sum(x * x, axis=-1, keepdims=True) + EPS)


def _to_chunks(t, C):
    B, S, H = t.shape[:3]
    t = t.reshape((B, S // C, C, H) + t.shape[3:])
    return jnp.moveaxis(t, 3, 1)


def _from_chunks(t):
    B, H, N, C = t.shape[:4]
    return jnp.moveaxis(t, 1, 3).reshape((B, N * C, H) + t.shape[4:])


def _scan_states(chunk_decay, kv):
    def step(S, inp):
        d, u = inp
        return d[..., None] * S + u, S
    S0 = jnp.zeros_like(kv[:, :, 0])
    _, states = lax.scan(step, S0, (jnp.moveaxis(chunk_decay, 2, 0), jnp.moveaxis(kv, 2, 0)))
    return jnp.moveaxis(states, 0, 2)


def rg_lru(x, w_a, b_a, w_x, b_x, lam):
    B, S, W = x.shape
    xb = x.reshape(B, S, LRU_BLOCKS, LRU_BLOCK)
    r = jax.nn.sigmoid(jnp.einsum('bsni,nij->bsnj', xb, w_a.astype(jnp.float32)).reshape(B, S, W) + b_a.astype(jnp.float32))
    i = jax.nn.sigmoid(jnp.einsum('bsni,nij->bsnj', xb, w_x.astype(jnp.float32)).reshape(B, S, W) + b_x.astype(jnp.float32))
    log_a = -LRU_C * r * jax.nn.softplus(-lam.astype(jnp.float32))
    a = jnp.exp(log_a)
    u = jnp.sqrt(-jnp.expm1(2.0 * log_a)) * (i * x)

    def combine(lhs, rhs):
        a1, b1 = lhs
        a2, b2 = rhs
        return a1 * a2, a2 * b1 + b2
    _, h = lax.associative_scan(combine, (a, u), axis=1)
    return h


def retention_chunkwise(q, k, v):
    H = q.shape[2]
    C = RET_CHUNK
    log_gamma = jnp.log1p(-jnp.power(2.0, -5.0 - jnp.arange(H, dtype=jnp.float32)))
    qc, kc, vc = _to_chunks(q, C), _to_chunks(k, C), _to_chunks(v, C)
    B, _, N = qc.shape[:3]
    idx = jnp.arange(C, dtype=jnp.float32)
    rel = idx[:, None] - idx[None, :]
    dmat = jnp.where(rel >= 0, jnp.exp(log_gamma[:, None, None] * jnp.maximum(rel, 0.0)), 0.0)
    scores = jnp.einsum('bhncd,bhnsd->bhncs', qc, kc) * dmat[None, :, None]
    o = jnp.einsum('bhncs,bhnse->bhnce', scores, vc)
    k_dec = jnp.exp(log_gamma[:, None] * (C - 1 - idx))
    kv = jnp.einsum('bhnsd,bhnse->bhnde', kc * k_dec[None, :, None, :, None], vc)
    chunk_decay = jnp.broadcast_to(jnp.exp(log_gamma * C)[None, :, None, None], (B, H, N, qc.shape[-1]))
    states = _scan_states(chunk_decay, kv)
    q_dec = jnp.exp(log_gamma[:, None] * (idx + 1.0))
    o = o + jnp.einsum('bhncd,bhnde->bhnce', qc * q_dec[None, :, None, :, None], states)
    return _from_chunks(o)


def hgrn2_chunkwise(q, k, v, log_f):
    C = HG_CHUNK
    mid = C // 2
    qc, kc, vc, lfc = _to_chunks(q, C), _to_chunks(k, C), _to_chunks(v, C), _to_chunks(log_f, C)
    b = jnp.cumsum(lfc, axis=3)
    b_mid = b[:, :, :, mid:mid + 1]
    qi = qc * jnp.exp(b - b_mid)
    ki = kc * jnp.exp(b_mid - b)
    causal = jnp.tril(jnp.ones((C, C), dtype=bool))
    scores = jnp.where(causal, jnp.einsum('bhncd,bhnsd->bhncs', qi, ki), 0.0)
    o = jnp.einsum('bhncs,bhnse->bhnce', scores, vc)
    b_last = b[:, :, :, -1:]
    kv = jnp.einsum('bhnsd,bhnse->bhnde', kc * jnp.exp(b_last - b), vc)
    states = _scan_states(jnp.exp(b_last[:, :, :, 0]), kv)
    o = o + jnp.einsum('bhncd,bhnde->bhnce', qc * jnp.exp(b), states)
    return _from_chunks(o)


def gated_delta_chunkwise(q, k, v, g, beta):
    C = GD_CHUNK
    qc, kc, vc = _to_chunks(q, C), _to_chunks(k, C), _to_chunks(v, C)
    gc, bc = _to_chunks(g, C), _to_chunks(beta, C)
    B, H, N = qc.shape[:3]
    dk, dv = qc.shape[-1], vc.shape[-1]
    gcum = jnp.cumsum(gc, axis=-1)
    incl = jnp.tril(jnp.ones((C, C), dtype=bool))
    strict = jnp.tril(jnp.ones((C, C), dtype=bool), k=-1)
    diff = gcum[..., :, None] - gcum[..., None, :]
    decay = jnp.where(incl, jnp.exp(jnp.where(incl, diff, 0.0)), 0.0)
    k_beta = kc * bc[..., None]
    v_beta = vc * bc[..., None]
    A = jnp.where(strict, jnp.einsum('bhncd,bhnsd->bhncs', k_beta, kc) * decay, 0.0)
    rhs = jnp.concatenate([v_beta, k_beta * jnp.exp(gcum)[..., None]], axis=-1)
    sol = lax.linalg.triangular_solve(jnp.eye(C, dtype=A.dtype) + A, rhs, left_side=True,
                                      lower=True, unit_diagonal=True)
    u, w = sol[..., :dv], sol[..., dv:]
    qk = jnp.where(incl, jnp.einsum('bhncd,bhnsd->bhncs', qc, kc) * decay, 0.0)
    q_g = qc * jnp.exp(gcum)[..., None]
    k_g = kc * jnp.exp(gcum[..., -1:] - gcum)[..., None]
    last = jnp.exp(gcum[..., -1])

    def step(S, inp):
        u_n, w_n, qk_n, q_n, k_n, d_n = inp
        v_new = u_n - jnp.einsum('bhcd,bhde->bhce', w_n, S)
        o_n = jnp.einsum('bhcd,bhde->bhce', q_n, S) + jnp.einsum('bhcs,bhse->bhce', qk_n, v_new)
        S = d_n[..., None, None] * S + jnp.einsum('bhsd,bhse->bhde', k_n, v_new)
        return S, o_n
    xs = tuple(jnp.moveaxis(t, 2, 0) for t in (u, w, qk, q_g, k_g, last))
    S0 = jnp.zeros((B, H, dk, dv), dtype=qc.dtype)
    _, o = lax.scan(step, S0, xs)
    return _from_chunks(jnp.moveaxis(o, 0, 2))


def even_mixer(h, positions, w_in, conv_w, conv_b, w_a, b_a, w_x, b_x, lam, w_out):
    B, S, _ = h.shape
    proj = (h @ w_in).astype(jnp.float32)
    xr, yr, q, k, v, g = _split(proj, EVEN_SIZES)
    xr = causal_depthwise_conv(xr, conv_w, conv_b)
    lru = rg_lru(xr, w_a, b_a, w_x, b_x, lam) * jax.nn.gelu(yr, approximate=True)
    q = rotary(q.reshape(B, S, RET_HEADS, RET_DK), positions)
    k = rotary(k.reshape(B, S, RET_HEADS, RET_DK), positions) * (RET_DK ** -0.5)
    o = retention_chunkwise(q, k, v.reshape(B, S, RET_HEADS, RET_DV))
    mu = jnp.mean(o, axis=-1, keepdims=True)
    var = jnp.mean(jnp.square(o - mu), axis=-1, keepdims=True)
    o = (o - mu) * lax.rsqrt(var + EPS)
    ret = o.reshape(B, S, RET_HEADS * RET_DV) * jax.nn.silu(g)
    mix = jnp.concatenate([lru, ret], axis=-1).astype(h.dtype)
    return mix @ w_out


def odd_mixer(h, lower_bound, w_in, hg_norm_w, conv_w, a_log, dt_bias, gd_norm_w, w_out):
    B, S, _ = h.shape
    proj = (h @ w_in).astype(jnp.float32)
    hq, hf, hi, hg, dq, dk_, dv_, dz, da, db = _split(proj, ODD_SIZES)
    lb = lower_bound.astype(jnp.float32)
    log_f = jnp.logaddexp(jnp.log(lb), jnp.log1p(-lb) + jax.nn.log_sigmoid(hf))
    key = -jnp.expm1(log_f)
    o_hg = hgrn2_chunkwise(hq.reshape(B, S, HG_HEADS, HG_DK), key.reshape(B, S, HG_HEADS, HG_DK),
                           hi.reshape(B, S, HG_HEADS, HG_DV), log_f.reshape(B, S, HG_HEADS, HG_DK))
    o_hg = rmsnorm(o_hg, hg_norm_w) * jax.nn.silu(hg.reshape(B, S, HG_HEADS, HG_DV))
    qkv = jax.nn.silu(causal_depthwise_conv(jnp.concatenate([dq, dk_, dv_], axis=-1), conv_w))
    dq, dk_, dv_ = _split(qkv, (GD_HEADS * GD_DK, GD_HEADS * GD_DK, GD_HEADS * GD_DV))
    q = l2norm(dq.reshape(B, S, GD_HEADS, GD_DK)) * (GD_DK ** -0.5)
    k = l2norm(dk_.reshape(B, S, GD_HEADS, GD_DK))
    beta = jax.nn.sigmoid(db)
    g = -jnp.exp(a_log.astype(jnp.float32)) * jax.nn.softplus(da + dt_bias.astype(jnp.float32))
    o_gd = gated_delta_chunkwise(q, k, dv_.reshape(B, S, GD_HEADS, GD_DV), g, beta)
    o_gd = rmsnorm(o_gd, gd_norm_w) * jax.nn.silu(dz.reshape(B, S, GD_HEADS, GD_DV))
    mix = jnp.concatenate([o_hg.reshape(B, S, -1), o_gd.reshape(B, S, -1)], axis=-1).astype(h.dtype)
    return mix @ w_out


def squared_relu_mlp(h, w1, w2):
    return jnp.square(jax.nn.relu(h @ w1)) @ w2


def hgrn_lower_bounds(logits):
    s = jax.nn.softmax(logits.astype(jnp.float32), axis=0)
    cs = jnp.cumsum(s, axis=0)
    return cs - cs[0:1]


def setup_inputs(seed: int = 0) -> dict:
    key = jax.random.key(seed)
    ks = jax.random.split(key, 32)
    nrm = lambda k, shape, s: jax.random.normal(k, shape, dtype=jnp.float32) * s
    D = D_MODEL
    x = nrm(ks[0], (BATCH, SEQ, D), 1.0)
    c = nrm(ks[1], (BATCH, D), 1.0)
    offset = jax.random.randint(ks[2], (BATCH, 1), 0, 1024, dtype=jnp.int32)
    positions = (offset + jnp.arange(SEQ, dtype=jnp.int32)[None, :]).astype(jnp.int32)
    ada_w = nrm(ks[3], (DEPTH, D, 6 * D), 0.3 * D ** -0.5)
    ada_b = nrm(ks[4], (DEPTH, 6 * D), 0.01)
    norm_mix_w = 1.0 + nrm(ks[5], (DEPTH, D), 0.05)
    norm_mlp_w = 1.0 + nrm(ks[6], (DEPTH, D), 0.05)
    mlp_w1 = nrm(ks[7], (DEPTH, D, MLP_HIDDEN), D ** -0.5)
    mlp_w2 = nrm(ks[8], (DEPTH, MLP_HIDDEN, D), MLP_HIDDEN ** -0.5)
    final_norm_w = 1.0 + nrm(ks[9], (D,), 0.05)
    ev_w_in = nrm(ks[10], (N_EVEN, D, EVEN_IN), D ** -0.5)
    lru_conv_w = nrm(ks[11], (N_EVEN, CONV_WIDTH, LRU_WIDTH), CONV_WIDTH ** -0.5)
    lru_conv_b = nrm(ks[12], (N_EVEN, LRU_WIDTH), 0.01)
    lru_w_a = nrm(ks[13], (N_EVEN, LRU_BLOCKS, LRU_BLOCK, LRU_BLOCK), LRU_BLOCK ** -0.5)
    lru_b_a = nrm(ks[14], (N_EVEN, LRU_WIDTH), 0.01)
    lru_w_x = nrm(ks[15], (N_EVEN, LRU_BLOCKS, LRU_BLOCK, LRU_BLOCK), LRU_BLOCK ** -0.5)
    lru_b_x = nrm(ks[16], (N_EVEN, LRU_WIDTH), 0.01)
    p = jax.random.uniform(ks[17], (N_EVEN, LRU_WIDTH), minval=0.9, maxval=0.999) ** (1.0 / LRU_C)
    lru_lambda = jnp.log(p) - jnp.log1p(-p)
    ev_w_out = nrm(ks[18], (N_EVEN, EVEN_MIX, D), EVEN_MIX ** -0.5)
    hg_lb_logits = nrm(ks[19], (DEPTH, HG_HEADS * HG_DK), 0.1)
    od_w_in = nrm(ks[20], (N_ODD, D, ODD_IN), D ** -0.5)
    hg_norm_w = 1.0 + nrm(ks[21], (N_ODD, HG_DV), 0.05)
    gd_conv_w = nrm(ks[22], (N_ODD, CONV_WIDTH, GD_HEADS * (2 * GD_DK + GD_DV)), CONV_WIDTH ** -0.5)
    gd_a_log = jnp.log(jax.random.uniform(ks[23], (N_ODD, GD_HEADS), minval=1.0, maxval=16.0))
    dt = jnp.exp(jax.random.uniform(ks[24], (N_ODD, GD_HEADS), minval=float(np.log(1e-3)), maxval=float(np.log(1e-1))))
    gd_dt_bias = dt + jnp.log(-jnp.expm1(-dt))
    gd_norm_w = 1.0 + nrm(ks[25], (N_ODD, GD_DV), 0.05)
    od_w_out = nrm(ks[26], (N_ODD, ODD_MIX, D), ODD_MIX ** -0.5)
    return {'x': x, 'c': c, 'positions': positions, 'ada_w': ada_w, 'ada_b': ada_b,
            'norm_mix_w': norm_mix_w, 'norm_mlp_w': norm_mlp_w, 'mlp_w1': mlp_w1, 'mlp_w2': mlp_w2,
            'final_norm_w': final_norm_w, 'ev_w_in': ev_w_in, 'lru_conv_w': lru_conv_w,
            'lru_conv_b': lru_conv_b, 'lru_w_a': lru_w_a, 'lru_b_a': lru_b_a, 'lru_w_x': lru_w_x,
            'lru_b_x': lru_b_x, 'lru_lambda': lru_lambda, 'ev_w_out': ev_w_out,
            'hg_lb_logits': hg_lb_logits, 'od_w_in': od_w_in, 'hg_norm_w': hg_norm_w,
            'gd_conv_w': gd_conv_w, 'gd_a_log': gd_a_log, 'gd_dt_bias': gd_dt_bias,
            'gd_norm_w': gd_norm_w, 'od_w_out': od_w_out}


def reference(x, c, positions, ada_w, ada_b, norm_mix_w, norm_mlp_w, mlp_w1, mlp_w2, final_norm_w,
              ev_w_in, lru_conv_w, lru_conv_b, lru_w_a, lru_b_a, lru_w_x, lru_b_x, lru_lambda, ev_w_out,
              hg_lb_logits, od_w_in, hg_norm_w, gd_conv_w, gd_a_log, gd_dt_bias, gd_norm_w, od_w_out):
    lower_bounds = hgrn_lower_bounds(hg_lb_logits)
    c_act = jax.nn.silu(c)
    for layer in range(DEPTH):
        mod = c_act @ ada_w[layer] + ada_b[layer]
        sh_m, sc_m, gt_m, sh_f, sc_f, gt_f = jnp.split(mod, 6, axis=-1)
        h = modulate(rmsnorm(x, norm_mix_w[layer]), sh_m, sc_m)
        j = layer // 2
        if layer % 2 == 0:
            y = even_mixer(h, positions, ev_w_in[j], lru_conv_w[j], lru_conv_b[j], lru_w_a[j], lru_b_a[j],
                           lru_w_x[j], lru_b_x[j], lru_lambda[j], ev_w_out[j])
        else:
            y = odd_mixer(h, lower_bounds[layer], od_w_in[j], hg_norm_w[j], gd_conv_w[j], gd_a_log[j],
                          gd_dt_bias[j], gd_norm_w[j], od_w_out[j])
        x = x + gt_m[:, None, :] * y
        h = modulate(rmsnorm(x, norm_mlp_w[layer]), sh_f, sc_f)
        x = x + gt_f[:, None, :] * squared_relu_mlp(h, mlp_w1[layer], mlp_w2[layer])
    return rmsnorm(x, final_norm_w)
```

```python
import contextlib
import numpy as np
import concourse.bass as bass
import concourse.mybir as mybir
from concourse.bass_utils import run_bass_kernel_spmd

F32 = mybir.dt.float32
BF16 = mybir.dt.bfloat16
I32 = mybir.dt.int32
AF = mybir.ActivationFunctionType
ALU = mybir.AluOpType

T = 2048
TB = 512
NB = T // TB
EPS = 1e-6


class Buf:
    __slots__ = ("name", "lw", "rs", "ap", "root", "excl")

    def __init__(self, name, ap=None, root=None, excl=False):
        self.name = name
        self.lw = None
        self.rs = []
        self.ap = ap
        self.root = root if root is not None else self
        self.excl = excl

    def __getitem__(self, k):
        return self.ap[k]


class Prog:
    ENGS = ("pe", "dve", "act", "pool", "sp")
    NRING = 16

    def __init__(self, nc):
        self.nc = nc
        self.ops = {e: [] for e in self.ENGS}
        self.nmile = {e: 0 for e in self.ENGS}
        self.waited = {e: {} for e in self.ENGS}
        self.dma_n = {e: 0 for e in self.ENGS}
        self.dma_cnt = {}
        self.stack = contextlib.ExitStack()
        self.stacks = [self.stack]
        self.nalloc = 0
        self.sems = {}
        keys = [e for e in self.ENGS if e != "sp"]
        keys += [("q", e, i) for e in ("sp", "act", "pool") for i in range(self.NRING)]
        for k in keys:
            nm = k if isinstance(k, str) else f"q_{k[1]}_{k[2]}"
            self.sems[k] = self.stack.enter_context(nc.semaphore("s_" + nm))

    @contextlib.contextmanager
    def scope(self):
        st = contextlib.ExitStack()
        self.stacks.append(st)
        with st:
            yield
            self.barrier()
            self.emit()
        self.stacks.pop()

    def sb(self, shape, dtype, name=None):
        self.nalloc += 1
        name = f"{name or 't'}_{self.nalloc}"
        t = self.stacks[-1].enter_context(self.nc.sbuf_tensor(name, list(shape), dtype))
        return Buf(name, t)

    def ps(self, shape, dtype=F32, name=None):
        self.nalloc += 1
        name = f"{name or 'p'}_{self.nalloc}"
        t = self.stacks[-1].enter_context(self.nc.psum_tensor(name, list(shape), dtype))
        return Buf(name, t, excl=True)

    def view(self, ap, name="v", parent=None, share=False):
        if parent is not None and (parent.excl or share):
            return Buf(name, ap, root=parent.root, excl=parent.excl)
        return Buf(name, ap)

    @staticmethod
    def _split(reads, writes):
        r2, w2 = [], []
        for b in reads:
            (w2 if b.excl else r2).append(b.root)
        for b in writes:
            w2.append(b.root)
        return r2, w2

    def _deps(self, eng, reads, writes):
        deps = {}

        def add(tok, kind):
            if tok is None:
                return
            key, val = tok
            if key == eng:
                if kind != "raw" or eng == "pe":
                    return
            if deps.get(key, 0) < val:
                deps[key] = val

        for b in reads:
            add(b.lw, "raw")
        for b in writes:
            add(b.lw, "waw")
            for r in b.rs:
                add(r, "war")
        w = self.waited[eng]
        out = []
        for key, val in deps.items():
            if isinstance(key, str):
                assert val <= self.nmile[key], f"wait on future milestone {key} {val} > {self.nmile[key]} (eng {eng})"
            if w.get(key, 0) >= val:
                continue
            w[key] = val
            out.append((key, val))
        return out

    def op(self, eng, meth, kw, reads=(), writes=(), inc=True):
        reads, writes = self._split(reads, writes)
        waits = self._deps(eng, reads, writes)
        if inc:
            self.nmile[eng] += 1
            tok = (eng, self.nmile[eng])
        else:
            tok = (eng, self.nmile[eng] + 1)
        for b in reads:
            b.rs.append(tok)
        for b in writes:
            b.lw = tok
            b.rs = []
        self.ops[eng].append((meth, kw, waits, inc, None))
        return tok

    def dma(self, eng, out, in_, reads=(), writes=(), **kw):
        reads, writes = self._split(reads, writes)
        waits = self._deps(eng, reads, writes)
        i = self.dma_n[eng]
        self.dma_n[eng] += 1
        key = ("q", eng, i % self.NRING)
        prev = self.dma_cnt.get(key, 0)
        if prev and self.waited[eng].get(key, 0) < prev:
            self.waited[eng][key] = prev
            waits.append((key, prev))
        self.dma_cnt[key] = prev + 16
        tok = (key, self.dma_cnt[key])
        for b in reads:
            b.rs.append(tok)
        for b in writes:
            b.lw = tok
            b.rs = []
        kw = dict(kw)
        kw["out"] = out
        kw["in_"] = in_
        self.ops[eng].append(("dma_start", kw, waits, False, key))
        return tok

    def wait_all(self, eng, toks):
        waits = []
        w = self.waited[eng]
        for key, val in toks:
            if w.get(key, 0) < val:
                w[key] = val
                waits.append((key, val))
        self.ops[eng].append((None, None, waits, False, None))

    def barrier(self, final=False):
        toks = [(e, self.nmile[e]) for e in self.ENGS if self.nmile[e] > 0]
        toks += [(k, v) for k, v in self.dma_cnt.items() if final or k[1] != "pool"]
        for e in self.ENGS:
            self.wait_all(e, [t for t in toks if t[0] != e])

    def emit(self):
        nc = self.nc
        sems = self.sems
        ops = self.ops
        self.ops = {e: [] for e in self.ENGS}
        with nc.Block() as block:

            def run(eng_name, e):
                for meth, kw, waits, inc, dkey in ops[eng_name]:
                    for key, val in waits:
                        e.wait_ge(sems[key], val)
                    if meth is None:
                        continue
                    inst = getattr(e, meth)(**kw)
                    if dkey is not None:
                        inst.then_inc(sems[dkey], 16)
                    elif inc:
                        inst.then_inc(sems[eng_name], 1)

            @block.tensor
            def _(e):
                run("pe", e)

            @block.vector
            def _(e):
                run("dve", e)

            @block.scalar
            def _(e):
                run("act", e)

            @block.gpsimd
            def _(e):
                run("pool", e)

            @block.sync
            def _(e):
                run("sp", e)


def _fm(v):
    v = np.asarray(v, np.float32)
    return np.ascontiguousarray(v.reshape(-1, 128).T)


class _Pack:
    def __init__(self):
        self.cols = []
        self.off = {}
        self.n = 0

    def add(self, name, arr):
        arr = np.asarray(arr, np.float32)
        assert arr.shape[0] == 128
        arr = arr.reshape(128, -1)
        self.off[name] = (self.n, arr.shape[1])
        self.cols.append(arr)
        self.n += arr.shape[1]

    def array(self):
        return np.ascontiguousarray(np.concatenate(self.cols, axis=1))


def _consts():
    pk = _Pack()
    i = np.arange(128)
    pk.add("ident", np.eye(128))
    pk.add("onesdiv", np.full((128, 128), 1.0 / 128))
    pk.add("ones", np.ones((128, 128)))
    lg = np.log1p(-np.power(2.0, -5.0 - np.arange(4)))
    rel = (i[None, :] - i[:, None]).astype(np.float64)
    for h in range(4):
        pk.add(f"rmask{h}", np.where(rel >= 0, np.exp(lg[h] * np.maximum(rel, 0)), 0.0))
    for h in range(4):
        pk.add(f"qdec{h}", np.broadcast_to(np.exp(lg[h] * (i + 1.0))[None, :], (128, 128)))
    pk.add("kdec", np.stack([np.exp(lg[h] * (127.0 - i)) for h in range(4)], 1))
    half = 64
    inv = 10000.0 ** (-(np.arange(half, dtype=np.float32)) / half)
    pk.add("invf", np.concatenate([inv, inv]).astype(np.float32)[:, None])
    pk.add("sgn", np.concatenate([-np.ones(64), np.ones(64)])[:, None])
    t = np.arange(512)
    pk.add("hrst", np.broadcast_to((t % 32 != 0).astype(np.float32)[None, :], (128, 512)))
    same32 = (i[:, None] // 32) == (i[None, :] // 32)
    pk.add("hmaskT", (same32 & (i[:, None] <= i[None, :])).astype(np.float32))
    same = (i[:, None] // 64) == (i[None, :] // 64)
    pk.add("gU", (same & (i[:, None] <= i[None, :])).astype(np.float32))
    pk.add("gMgt", (same & (i[:, None] > i[None, :])).astype(np.float32))
    pk.add("gMincl", (same & (i[None, :] <= i[:, None])).astype(np.float32))
    pk.add("gMinclT", (same & (i[None, :] >= i[:, None])).astype(np.float32))
    pk.add("gMstrict", (same & (i[None, :] < i[:, None])).astype(np.float32))
    pk.add("gselA", np.broadcast_to((i < 64).astype(np.float32)[:, None], (128, 128)))
    pk.add("gselB", np.broadcast_to((i >= 64).astype(np.float32)[:, None], (128, 128)))
    pk.add("gsame", same.astype(np.float32))
    pk.add("gcolA", np.broadcast_to((i < 64).astype(np.float32)[None, :], (128, 128)))
    pk.add("gcolB", np.broadcast_to((i >= 64).astype(np.float32)[None, :], (128, 128)))
    return pk


_CST = _consts()
LRET = [float(np.log1p(-2.0 ** (-5.0 - h))) for h in range(4)]


def _params(inp):
    pk = _Pack()
    for l in range(2):
        pk.add(f"nmw{l}", _fm(inp["norm_mix_w"][l]))
        pk.add(f"nfw{l}", _fm(inp["norm_mlp_w"][l]))
    pk.add("fnw", _fm(inp["final_norm_w"]))
    for j in range(4):
        pk.add(f"cw{j}", _fm(inp["lru_conv_w"][0][j]))
    pk.add("cb", _fm(inp["lru_conv_b"][0]))
    pk.add("ba", _fm(inp["lru_b_a"][0]))
    pk.add("bx", _fm(inp["lru_b_x"][0]))
    pk.add("lam", _fm(inp["lru_lambda"][0]))
    pk.add("lb0", _fm(inp["hg_lb_logits"][0]))
    pk.add("lb1", _fm(inp["hg_lb_logits"][1]))
    for j in range(4):
        pk.add(f"gcw{j}", _fm(inp["gd_conv_w"][0][j]))
    pk.add("hgw", _fm(inp["hg_norm_w"][0]))
    pk.add("gdw", _fm(inp["gd_norm_w"][0]))
    return pk


import os
CUT = int(os.environ.get('CUT', '9'))


class Builder:
    def __init__(self, poff, npv, nseq=2, nlayers=2, skip_mixer=False):
        self.poff = poff
        self.nseq = nseq
        self.nlayers = nlayers
        self.skip_mixer = skip_mixer
        nc = self.nc = bass.Bass("TRN2", target_bir_lowering=False)
        self.P = Prog(nc)
        din = lambda n, s, d=F32: nc.dram_tensor(n, list(s), d, kind="ExternalInput").ap()
        self.xT = din("xT", [nseq, 1024, T])
        self.nsm = max(nseq, 2)
        self.cT = din("cT", [128, 8 * self.nsm])
        self.pos = din("pos", [nseq, T], I32)
        self.ada_w = din("ada_w", [2, 1024, 6144])
        self.ada_b = din("ada_b", [2, 6144])
        self.d_w = {
            "w0": din("w0", [1024, 4096]), "wo0": din("wo0", [1024, 1024]),
            "w1i": din("w1i", [1024, 4096]), "wo1": din("wo1", [1024, 1024]),
            "m1_0": din("m1_0", [1024, 4096]), "m2_0": din("m2_0", [4096, 1024]),
            "m1_1": din("m1_1", [1024, 4096]), "m2_1": din("m2_1", [4096, 1024]),
        }
        self.wab = din("wab", [128, 64])
        self.pv_d = din("pv", [128, npv])
        self.cst_d = din("cst", [128, _CST.n])
        self.wbd_d = din("wbd", [128, 8 * 128])
        self.gdp_d = din("gdp", [1, 8])
        self.outT = nc.dram_tensor("outT", [nseq, 1024, T], F32, kind="ExternalOutput").ap()
        self.npv = npv

    def I(self, eng, meth, R, W, inc=True, **kw):
        return self.P.op(eng, meth, kw, R, W, inc)

    def C(self, name):
        o, n = _CST.off[name]
        return self.cst[:, o:o + n]

    def pvs(self, name, j=None):
        o, n = self.poff[name]
        if j is None:
            return self.pv[:, o:o + n]
        return self.pv[:, o + j:o + j + 1]

    def mm(self, ps, out, lhsT, rhs, R, start=True, stop=True, inc=None):
        if inc is None:
            inc = stop
        return self.I("pe", "matmul", R, [ps], inc=inc, out=out, lhsT=lhsT, rhs=rhs, start=start, stop=stop)

    def big(self):
        b = self.pbig[self.ibig % len(self.pbig)]
        self.ibig += 1
        return b

    def small(self):
        b = self.psmall[self.ismall % len(self.psmall)]
        self.ismall += 1
        return b

    def tps(self):
        b = self.ptp[self.itp % len(self.ptp)]
        self.itp += 1
        return b

    def alloc_psum(self, nbig, nsmall_banks, ntp_banks):
        P = self.P
        self.pbig = [P.ps([128, TB], F32, "pbig") for _ in range(nbig)]
        self.ibig = 0
        banks = [P.ps([128, TB], F32, "psm") for _ in range(nsmall_banks)]
        self.psmall = [P.view(b[:, q * 128:(q + 1) * 128], "psmv", parent=b) for q in range(4) for b in banks]
        self.ismall = 0
        banks = [P.ps([128, 1024], BF16, "ptp") for _ in range(ntp_banks)]
        self.ptp = [P.view(b[:, q * 128:(q + 1) * 128], "ptpv", parent=b) for q in range(4) for b in banks]
        self.itp = 0

    def wload(self, name, ti):
        P = self.P
        buf = self.wring[self.wi % len(self.wring)]
        self.wi += 1
        scr, tb = self.wt[name]
        P.dma("sp", buf[:], scr[ti], reads=[tb[ti]], writes=[buf])
        return buf, buf[:].rearrange("p (k n) -> p k n", k=8)

    def proj_fm(self, w, hT, j, ps):
        wb, w3 = w
        for k in range(8):
            self.mm(ps, ps[:], w3[:, k, j * 128:(j + 1) * 128], hT[k][:], [wb, hT[k]], start=(k == 0), stop=(k == 7))

    def proj_tm(self, w, hT, i, ps):
        wb, w3 = w
        for k in range(8):
            self.mm(ps, ps[:], hT[k][:, i * 128:(i + 1) * 128], w3[:, k, :], [wb, hT[k]], start=(k == 0), stop=(k == 7))

    def nm_alloc(self):
        P = self.P
        return {
            "sq": [P.sb([128, TB], BF16, "nsq") for _ in range(2)],
            "rs": P.sb([128, TB], F32, "nrs"),
            "tmp": [P.sb([128, TB], F32, "ntmp") for _ in range(2)],
        }

    def norm_stats(self, nm, tb):
        ps = nm["ps"] if "ps" in nm else self.big()
        for k in range(8):
            sq = nm["sq"][k % 2]
            self.I("act", "activation", [self.xs[k][tb]], [sq], out=sq[:], in_=self.xs[k][tb][:], func=AF.Square)
            self.mm(ps, ps[:], self.onesb, sq[:], [sq, self.cstb], start=(k == 0), stop=(k == 7), inc=True)
        rs = nm["rs"]
        self.I("act", "activation", [ps], [rs], out=rs[:], in_=ps[:], func=AF.Ln, scale=1.0 / 1024, bias=self.epsc[:, 0:1])
        self.I("act", "activation", [rs], [rs], out=rs[:], in_=rs[:], func=AF.Exp, scale=-0.5)
        return rs

    def norm_mod(self, nm, tb, A, Bv, hT):
        rs = self.norm_stats(nm, tb)
        for k in range(8):
            tmp = nm["tmp"][k % 2]
            self.I("dve", "scalar_tensor_tensor", [self.xs[k][tb], rs], [tmp], out=tmp[:], in0=self.xs[k][tb][:],
                   scalar=A(k), in1=rs[:], op0=ALU.mult, op1=ALU.mult)
            self.I("act", "activation", [tmp, self.modT[0]], [hT[k]], out=hT[k][:], in_=tmp[:], func=AF.Identity, bias=Bv(k))

    def setup(self):
        P = self.P
        nseq, nl, nsm = self.nseq, self.nlayers, self.nsm
        self.cst = P.sb([128, _CST.n], F32, "cst")
        P.dma("sp", self.cst[:], self.cst_d, writes=[self.cst])
        self.cstb = P.sb([128, 384], BF16, "cstb")
        P.dma("pool", self.cstb[:], self.cst_d[:, 0:384], writes=[self.cstb])
        self.identb = self.cstb[:, 0:128]
        self.onesdivb = self.cstb[:, 128:256]
        self.onesb = self.cstb[:, 256:384]
        self.identf = self.C("ident")
        self.colA = self.C("gcolA")
        self.colB = self.C("gcolB")
        self.pv = P.sb([128, self.npv], F32, "pv")
        P.dma("sp", self.pv[:], self.pv_d, writes=[self.pv])
        self.epsc = P.sb([128, 1], F32, "epsc")
        self.I("dve", "memset", [], [self.epsc], ap=self.epsc[:], constant=EPS)
        self.onep = P.sb([128, 1], F32, "onep")
        self.I("dve", "memset", [], [self.onep], ap=self.onep[:], constant=1.0000001)
        self.wt = {}
        order = ["w0", "wo0", "m1_0", "m2_0", "w1i", "wo1", "m1_1", "m2_1"]
        if nl == 1:
            order = order[:4]

        def convert(name):
            src = self.d_w[name]
            K, N = src.shape
            kt, nt = K // 1024, N // 512
            scr = self.nc.dram_tensor(name + "_bf", [kt * nt, 128, 4096], BF16, kind="Internal").ap()
            srcv = src.rearrange("(kk p) n -> p kk n", p=128)
            tb = []
            for kg in range(kt):
                for ng in range(nt):
                    b = Buf(f"{name}_{kg}_{ng}")
                    P.dma("pool", scr[kg * nt + ng].rearrange("p (k n) -> p k n", k=8),
                          srcv[:, kg * 8:(kg + 1) * 8, ng * 512:(ng + 1) * 512], writes=[b])
                    tb.append(b)
            self.wt[name] = (scr, tb)

        for name in order[:4]:
            convert(name)
        self.late_conv = lambda: [convert(n) for n in order[4:]]
        self.wring = [P.sb([128, 4096], BF16, "wring") for _ in range(4)]
        self.wi = 0
        self.xs_t = [P.sb([128, T], F32, "xs") for _ in range(8)]
        self.xs = [[P.view(self.xs_t[k][:, tb * TB:(tb + 1) * TB], "xsv") for tb in range(NB)] for k in range(8)]
        self.modT = [P.sb([128, 48 * nsm], F32, "modT") for _ in range(nl)]
        self.AM = P.sb([128, nl * nseq * 16], F32, "AM")
        self.lruc = P.sb([128, 8], F32, "lruc")
        with P.scope():
            ca = P.sb([128, 8 * nsm], F32, "ca")
            P.dma("sp", ca[:], self.cT, writes=[ca])
            self.I("act", "activation", [ca], [ca], out=ca[:], in_=ca[:], func=AF.Silu)
            ca3 = ca[:].rearrange("p (k s) -> p k s", s=nsm)
            awr = [P.sb([128, 8, 512], F32, "awr") for _ in range(2)]
            mod2 = P.sb([nsm, 6144], F32, "mod2")
            bb = P.sb([nsm, 6144], F32, "bb")
            psr = [P.ps([128, TB], F32, "psr") for _ in range(2)]
            pst = P.ps([128, 48 * nsm], F32, "pst")
            for l in range(nl):
                P.dma("sp", bb[:], self.ada_b[l:l + 1, :].partition_broadcast(nsm), writes=[bb])
                awv = self.ada_w[l].rearrange("(k p) n -> p k n", p=128)
                for n in range(12):
                    aw = awr[n % 2]
                    P.dma("sp", aw[:], awv[:, :, n * 512:(n + 1) * 512], writes=[aw])
                    ps = psr[n % 2]
                    for k in range(8):
                        self.mm(ps, ps[0:nsm, :], ca3[:, k, :], aw[:, k, :], [ca, aw], start=(k == 0), stop=(k == 7))
                    self.I("dve", "tensor_tensor", [ps, bb], [mod2], out=mod2[:, n * 512:(n + 1) * 512], in0=ps[0:nsm, :],
                           in1=bb[:, n * 512:(n + 1) * 512], op=ALU.add)
                for j in range(48):
                    self.I("pe", "transpose", [mod2, self.cst], [pst], out=pst[:, j * nsm:(j + 1) * nsm],
                           in_=mod2[0:nsm, j * 128:(j + 1) * 128], identity=self.identf[0:nsm, 0:nsm])
                self.I("dve", "tensor_copy", [pst], [self.modT[l]], out=self.modT[l][:], in_=pst[:])
                mv = self.modT[l][:].rearrange("p (j s) -> p j s", s=nsm)
                for s in range(nseq):
                    o = (l * nseq + s) * 16
                    self.I("dve", "scalar_tensor_tensor", [self.modT[l], self.pv], [self.AM], out=self.AM[:, o:o + 8],
                           in0=mv[:, 8:16, s], scalar=1.0, in1=self.pvs(f"nmw{l}"), op0=ALU.add, op1=ALU.mult)
                    self.I("dve", "scalar_tensor_tensor", [self.modT[l], self.pv], [self.AM], out=self.AM[:, o + 8:o + 16],
                           in0=mv[:, 32:40, s], scalar=1.0, in1=self.pvs(f"nfw{l}"), op0=ALU.add, op1=ALU.mult)
            sp = P.sb([128, 4], F32, "sp")
            self.I("act", "activation", [self.pv], [sp], out=sp[:], in_=self.pvs("lam"), func=AF.Exp, scale=-1.0)
            self.I("act", "activation", [sp], [sp], out=sp[:], in_=sp[:], func=AF.Ln, bias=1.0)
            self.I("dve", "tensor_scalar", [sp], [self.lruc], out=self.lruc[:, 0:4], in0=sp[:], scalar1=-8.0, scalar2=None, op0=ALU.mult)
            self.I("dve", "tensor_scalar", [sp], [self.lruc], out=self.lruc[:, 4:8], in0=sp[:], scalar1=-16.0, scalar2=None, op0=ALU.mult)

    def modv(self, l, s, g, k):
        c = (g * 8 + k) * self.nsm + s
        return self.modT[l][:, c:c + 1]

    def amv(self, l, s, which, k):
        o = (l * self.nseq + s) * 16 + which * 8 + k
        return self.AM[:, o:o + 1]

    def trig_tables(self, posf_ap, posbuf, cosT, sinT, scr):
        ki, kf, a = scr
        TWO_PI = float(2 * np.pi)
        for tab, shift in ((sinT, float(np.pi)), (cosT, float(1.5 * np.pi))):
            self.I("dve", "tensor_scalar", [posbuf, self.cst], [a], out=a[:], in0=posf_ap, scalar1=self.C("invf"), scalar2=shift, op0=ALU.mult, op1=ALU.add)
            self.I("dve", "tensor_scalar", [a], [ki], out=ki[:], in0=a[:], scalar1=1.0 / TWO_PI, scalar2=None, op0=ALU.mult)
            self.I("dve", "tensor_copy", [ki], [kf], out=kf[:], in_=ki[:])
            self.I("dve", "scalar_tensor_tensor", [kf, a], [a], out=a[:], in0=kf[:], scalar=-TWO_PI, in1=a[:], op0=ALU.mult, op1=ALU.add)
            self.I("dve", "tensor_scalar", [a], [kf], out=kf[:], in0=a[:], scalar1=0.0, scalar2=TWO_PI, op0=ALU.is_lt, op1=ALU.mult)
            self.I("dve", "tensor_tensor", [a, kf], [a], out=a[:], in0=a[:], in1=kf[:], op=ALU.add)
            self.I("dve", "tensor_scalar", [a], [kf], out=kf[:], in0=a[:], scalar1=TWO_PI, scalar2=-TWO_PI, op0=ALU.is_ge, op1=ALU.mult)
            self.I("dve", "tensor_tensor", [a, kf], [a], out=a[:], in0=a[:], in1=kf[:], op=ALU.add)
            self.I("dve", "tensor_scalar", [a], [a], out=a[:], in0=a[:], scalar1=-float(np.pi), scalar2=-3.1415925, op0=ALU.add, op1=ALU.max)
            self.I("dve", "tensor_scalar", [a], [a], out=a[:], in0=a[:], scalar1=3.1415925, scalar2=None, op0=ALU.min)
            self.I("act", "activation", [a], [tab], out=tab[:], in_=a[:], func=AF.Sin)
        self.I("dve", "tensor_scalar", [sinT, self.cst], [sinT], out=sinT[:], in0=sinT[:], scalar1=self.C("sgn"), scalar2=None, op0=ALU.mult)

    def out_proj(self, wname, mix, l, s, tb):
        for ng in range(2):
            wb, w3 = self.wload(wname, ng)
            for mi in range(4):
                m = ng * 4 + mi
                ps = self.big()
                for c in range(8):
                    self.mm(ps, ps[:], w3[:, c, mi * 128:(mi + 1) * 128], mix[c][:], [wb, mix[c]], start=(c == 0), stop=(c == 7))
                xb = self.xs[m][tb]
                self.I("dve", "scalar_tensor_tensor", [ps, xb], [xb], out=xb[:], in0=ps[:], scalar=self.modv(l, s, 2, m), in1=xb[:], op0=ALU.mult, op1=ALU.add)

    def mixer0(self, s):
        P = self.P
        I = self.I
        with P.scope():
            nm = self.nm_alloc()
            self.alloc_psum(3, 3, 2)
            sb = P.sb
            hT = [sb([128, TB], BF16, "hT") for _ in range(8)]
            mix = [sb([128, TB], BF16, "mix") for _ in range(8)]
            xh = [sb([128, TB + 3], F32, "xh") for _ in range(4)]
            hc = [sb([128, 1], F32, "hc") for _ in range(4)]
            for c in range(4):
                I("pool", "memset", [], [xh[c]], ap=xh[c][:, 0:3], constant=0.0)
                I("pool", "memset", [], [hc[c]], ap=hc[c][:], constant=0.0)
            F = [sb([128, TB], F32, f"f{i}") for i in range(8)]
            xcb = sb([128, TB], BF16, "xcb")
            wbd = sb([128, 8, 128], BF16, "wbd")
            P.dma("pool", wbd[:], self.wbd_d.rearrange("p (c n) -> p c n", c=8), writes=[wbd])
            S32 = [[sb([128, 128], F32, "S32") for _ in range(2)] for _ in range(4)]
            Sbs = [[sb([128, 128], BF16, "Sbs") for _ in range(6)] for _ in range(4)]
            for h in range(4):
                I("pool", "memset", [], [S32[h][0]], ap=S32[h][0][:], constant=0.0)
                I("pool", "memset", [], [Sbs[h][0]], ap=Sbs[h][0][:], constant=0.0)
            posi = sb([128, TB], I32, "posi")
            posf = sb([128, TB], F32, "posf")
            cosT = sb([128, TB], F32, "cosT")
            sinT = sb([128, TB], F32, "sinT")
            ki = posi
            qb = [sb([128, TB], BF16, "qb") for _ in range(4)]
            qd = [sb([128, TB], BF16, "qd") for _ in range(4)]
            kb = [sb([128, TB], BF16, "kb") for _ in range(4)]
            sg = [sb([128, TB], BF16, "sg") for _ in range(4)]
            vtm = [sb([128, TB], BF16, "vtm") for _ in range(4)]
            ktm = [[sb([128, 128], BF16, "ktm") for _ in range(4)] for _ in range(4)]
            pt = [sb([128, 128], BF16, "pt") for _ in range(4)]
            obf = sb([128, TB], BF16, "obf")
            osq = sb([128, TB], BF16, "osq")
            c1 = lambda c: self.lruc[:, c:c + 1]
            c2 = lambda c: self.lruc[:, 4 + c:5 + c]
            G128 = [float(np.exp(LRET[h] * 128.0)) for h in range(4)]
            for tb in range(NB):
                self.norm_mod(nm, tb, lambda k: self.amv(0, s, 0, k), lambda k: self.modv(0, s, 0, k), hT)
                wxr = self.wload("w0", 0)
                wyr = self.wload("w0", 1)
                for c in range(4):
                    xc, yv, r, ig, a, sq, hh, t1 = F
                    psx = self.big()
                    self.proj_fm(wxr, hT, c, psx)
                    I("act", "activation", [psx], [xh[c]], out=xh[c][:, 3:3 + TB], in_=psx[:], func=AF.Identity)
                    psy = self.big()
                    self.proj_fm(wyr, hT, c, psy)
                    I("act", "activation", [psy], [yv], out=yv[:], in_=psy[:], func=AF.Identity)
                    I("act", "activation", [psx, self.pv], [xc], out=xc[:], in_=psx[:], func=AF.Identity, scale=self.pvs("cw3", c), bias=self.pvs("cb", c))
                    for j in range(3):
                        I("dve", "scalar_tensor_tensor", [xh[c], xc, self.pv], [xc], out=xc[:], in0=xh[c][:, j:j + TB], scalar=self.pvs(f"cw{j}", c), in1=xc[:], op0=ALU.mult, op1=ALU.add)
                    I("dve", "tensor_copy", [xh[c]], [xh[c]], out=xh[c][:, 0:3], in_=xh[c][:, TB:TB + 3])
                    I("act", "activation", [xc], [xcb], out=xcb[:], in_=xc[:], func=AF.Identity)
                    psa = self.big()
                    self.mm(psa, psa[:], wbd[:, c, :], xcb[:], [wbd, xcb])
                    I("act", "activation", [psa, self.pv], [r], out=r[:], in_=psa[:], func=AF.Sigmoid, bias=self.pvs("ba", c))
                    psi = self.big()
                    self.mm(psi, psi[:], wbd[:, 4 + c, :], xcb[:], [wbd, xcb])
                    I("act", "activation", [psi, self.pv], [ig], out=ig[:], in_=psi[:], func=AF.Sigmoid, bias=self.pvs("bx", c))
                    I("act", "activation", [r, self.lruc], [a], out=a[:], in_=r[:], func=AF.Exp, scale=c1(c))
                    I("act", "activation", [r, self.lruc], [sq], out=sq[:], in_=r[:], func=AF.Exp, scale=c2(c))
                    I("act", "activation", [sq], [sq], out=sq[:], in_=sq[:], func=AF.Ln, scale=-1.0, bias=self.onep[:, 0:1])
                    I("act", "activation", [sq], [sq], out=sq[:], in_=sq[:], func=AF.Exp, scale=0.5)
                    I("dve", "tensor_tensor", [ig, xc], [ig], out=ig[:], in0=ig[:], in1=xc[:], op=ALU.mult)
                    I("dve", "tensor_tensor", [ig, sq], [ig], out=ig[:], in0=ig[:], in1=sq[:], op=ALU.mult)
                    I("dve", "tensor_tensor_scan", [a, ig, hc[c]], [hh], out=hh[:], data0=a[:], data1=ig[:], initial=hc[c][:, 0:1], op0=ALU.mult, op1=ALU.add)
                    I("dve", "tensor_copy", [hh], [hc[c]], out=hc[c][:, 0:1], in_=hh[:, TB - 1:TB])
                    I("pool", "tensor_tensor", [yv], [t1], out=t1[:], in0=yv[:], in1=yv[:], op=ALU.mult)
                    I("pool", "tensor_scalar", [t1], [t1], out=t1[:], in0=t1[:], scalar1=0.044715, scalar2=1.0, op0=ALU.mult, op1=ALU.add)
                    I("pool", "tensor_tensor", [t1, yv], [t1], out=t1[:], in0=t1[:], in1=yv[:], op=ALU.mult)
                    I("act", "activation", [t1], [t1], out=t1[:], in_=t1[:], func=AF.Sigmoid, scale=1.5957691216057308)
                    I("pool", "tensor_tensor", [t1, yv], [t1], out=t1[:], in0=t1[:], in1=yv[:], op=ALU.mult)
                    I("dve", "tensor_tensor", [t1, hh], [mix[c]], out=mix[c][:], in0=t1[:], in1=hh[:], op=ALU.mult)
                if CUT == 1:
                    continue
                t1, t2, o32, m2 = F[0], F[1], F[2], F[3]
                P.dma("sp", posi[:], self.pos[s:s + 1, tb * TB:(tb + 1) * TB].partition_broadcast(128), writes=[posi])
                I("dve", "tensor_copy", [posi], [posf], out=posf[:], in_=posi[:])
                self.trig_tables(posf[:], posf, cosT, sinT, (ki, F[4], F[5]))
                if CUT == 2:
                    continue
                wq = self.wload("w0", 2)
                wqs = self.wload("w0", 6)
                for h in range(4):
                    psq = self.big()
                    self.proj_fm(wq, hT, h, psq)
                    psqs = self.big()
                    self.proj_fm(wqs, hT, h, psqs)
                    I("dve", "tensor_tensor", [psq, cosT], [t1], out=t1[:], in0=psq[:], in1=cosT[:], op=ALU.mult)
                    I("dve", "tensor_tensor", [psqs, sinT], [t2], out=t2[:], in0=psqs[:], in1=sinT[:], op=ALU.mult)
                    I("dve", "tensor_tensor", [t1, t2], [t1], out=t1[:], in0=t1[:], in1=t2[:], op=ALU.add)
                    I("act", "activation", [t1], [qb[h]], out=qb[h][:], in_=t1[:], func=AF.Identity)
                    I("pool", "tensor_tensor", [t1, self.cst], [qd[h]], out=qd[h][:].rearrange("p (a b) -> p a b", a=4),
                      in0=t1[:].rearrange("p (a b) -> p a b", a=4), in1=self.C(f"qdec{h}").unsqueeze(1).to_broadcast([128, 4, 128]), op=ALU.mult)
                wk = self.wload("w0", 3)
                wks = self.wload("w0", 7)
                for h in range(4):
                    psk = self.big()
                    self.proj_fm(wk, hT, h, psk)
                    psks = self.big()
                    self.proj_fm(wks, hT, h, psks)
                    I("dve", "scalar_tensor_tensor", [psk, cosT], [t1], out=t1[:], in0=psk[:], scalar=128.0 ** -0.5, in1=cosT[:], op0=ALU.mult, op1=ALU.mult)
                    I("dve", "scalar_tensor_tensor", [psks, sinT], [t2], out=t2[:], in0=psks[:], scalar=128.0 ** -0.5, in1=sinT[:], op0=ALU.mult, op1=ALU.mult)
                    I("dve", "tensor_tensor", [t1, t2], [kb[h]], out=kb[h][:], in0=t1[:], in1=t2[:], op=ALU.add)
                wv = self.wload("w0", 4)
                for i in range(4):
                    psv = self.big()
                    self.proj_tm(wv, hT, i, psv)
                    I("act", "activation", [psv], [vtm[i]], out=vtm[i][:], in_=psv[:], func=AF.Identity)
                wg = self.wload("w0", 5)
                for h in range(4):
                    psg = self.big()
                    self.proj_fm(wg, hT, h, psg)
                    I("act", "activation", [psg], [sg[h]], out=sg[h][:], in_=psg[:], func=AF.Silu)
                if CUT == 3:
                    continue
                for h in range(4):
                    for i in range(4):
                        tp = self.tps()
                        I("pe", "transpose", [kb[h], self.cstb], [tp], out=tp[:], in_=kb[h][:, i * 128:(i + 1) * 128], identity=self.identb)
                        I("act", "activation", [tp, self.cst], [ktm[h][i]], out=ktm[h][i][:], in_=tp[:], func=AF.Identity, scale=self.C("kdec")[:, h:h + 1])
                if CUT == 4:
                    continue
                for i in range(4):
                    gi = tb * 4 + i
                    for h in range(4):
                        kvp = self.small()
                        self.mm(kvp, kvp[:], ktm[h][i][:], vtm[i][:, h * 128:(h + 1) * 128], [ktm[h][i], vtm[i]])
                        so, sn = S32[h][gi % 2], S32[h][(gi + 1) % 2]
                        bn = Sbs[h][(gi + 1) % 6]
                        I("dve", "scalar_tensor_tensor", [so, kvp], [sn], out=sn[:], in0=so[:], scalar=G128[h], in1=kvp[:], op0=ALU.mult, op1=ALU.add)
                        I("dve", "scalar_tensor_tensor", [so, kvp], [bn], out=bn[:], in0=so[:], scalar=G128[h], in1=kvp[:], op0=ALU.mult, op1=ALU.add)
                if CUT == 5:
                    continue
                for h in range(4):
                    scs = []
                    for i in range(4):
                        sc = self.small()
                        self.mm(sc, sc[:], kb[h][:, i * 128:(i + 1) * 128], qb[h][:, i * 128:(i + 1) * 128], [kb[h], qb[h]])
                        I("dve", "tensor_tensor", [sc, self.cst], [pt[i]], out=pt[i][:], in0=sc[:], in1=self.C(f"rmask{h}"), op=ALU.mult)
                    ob = self.big()
                    for i in range(4):
                        gi = tb * 4 + i
                        bo = Sbs[h][gi % 6]
                        self.mm(ob, ob[:, i * 128:(i + 1) * 128], vtm[i][:, h * 128:(h + 1) * 128], pt[i][:], [vtm[i], pt[i]], start=True, stop=False, inc=False)
                        self.mm(ob, ob[:, i * 128:(i + 1) * 128], bo[:], qd[h][:, i * 128:(i + 1) * 128], [bo, qd[h]], start=False, stop=True)
                    I("act", "activation", [ob], [o32], out=o32[:], in_=ob[:], func=AF.Identity)
                    I("pool", "tensor_copy", [o32], [obf], out=obf[:], in_=o32[:])
                    I("act", "activation", [o32], [osq], out=osq[:], in_=o32[:], func=AF.Square)
                    pm = self.big()
                    self.mm(pm, pm[:], self.onesdivb, obf[:], [obf, self.cstb])
                    pq = self.big()
                    self.mm(pq, pq[:], self.onesdivb, osq[:], [osq, self.cstb])
                    I("act", "activation", [pm], [m2], out=m2[:], in_=pm[:], func=AF.Square)
                    I("dve", "tensor_tensor", [pq, m2], [m2], out=m2[:], in0=pq[:], in1=m2[:], op=ALU.subtract)
                    I("dve", "tensor_scalar", [m2], [m2], out=m2[:], in0=m2[:], scalar1=0.0, scalar2=None, op0=ALU.max)
                    I("act", "activation", [m2], [m2], out=m2[:], in_=m2[:], func=AF.Ln, bias=self.epsc[:, 0:1])
                    I("act", "activation", [m2], [m2], out=m2[:], in_=m2[:], func=AF.Exp, scale=-0.5)
                    I("dve", "tensor_tensor", [o32, pm], [o32], out=o32[:], in0=o32[:], in1=pm[:], op=ALU.subtract)
                    I("dve", "tensor_tensor", [o32, m2], [o32], out=o32[:], in0=o32[:], in1=m2[:], op=ALU.mult)
                    I("dve", "tensor_tensor", [o32, sg[h]], [mix[4 + h]], out=mix[4 + h][:], in0=o32[:], in1=sg[h][:], op=ALU.mult)
                off = os.environ.get("M0OFF", "")
                for c in range(8):
                    if (off == "lru" and c < 4) or (off == "ret" and c >= 4):
                        I("pool", "memset", [], [mix[c]], ap=mix[c][:], constant=0.0)
                self.out_proj("wo0", mix, 0, s, tb)

    def conv_silu(self, ps, work, carry, wcol, out):
        I = self.I
        I("act", "activation", [ps], [work], out=work[:, 3:3 + TB], in_=ps[:], func=AF.Identity)
        I("dve", "tensor_copy", [carry], [work], out=work[:, 0:3], in_=carry[:])
        I("act", "activation", [ps, self.pv], [out], out=out[:], in_=ps[:], func=AF.Identity, scale=self.pvs("gcw3", wcol))
        for j in range(3):
            I("dve", "scalar_tensor_tensor", [work, out, self.pv], [out], out=out[:], in0=work[:, j:j + TB], scalar=self.pvs(f"gcw{j}", wcol), in1=out[:], op0=ALU.mult, op1=ALU.add)
        I("dve", "tensor_copy", [work], [carry], out=carry[:], in_=work[:, TB:TB + 3])
        I("act", "activation", [out], [out], out=out[:], in_=out[:], func=AF.Silu)

    def rstd_bc(self, src, sqb, dst, lhsT):
        I = self.I
        I("act", "activation", [src], [sqb], out=sqb[:], in_=src[:], func=AF.Square)
        ps = self.big()
        self.mm(ps, ps[:], lhsT, sqb[:], [sqb, self.cstb])
        I("act", "activation", [ps], [dst], out=dst[:], in_=ps[:], func=AF.Ln, bias=self.epsc[:, 0:1])
        I("act", "activation", [dst], [dst], out=dst[:], in_=dst[:], func=AF.Exp, scale=-0.5)

    def mixer1(self, s):
        P = self.P
        I = self.I
        with P.scope():
            nm = self.nm_alloc()
            self.alloc_psum(4, 2, 2)
            sb = P.sb
            hTs = [[sb([128, TB], BF16, "hT") for _ in range(8)] for _ in range(2)]
            mix = [sb([128, TB], BF16, "mix") for _ in range(8)]
            F = [sb([128, TB], F32, f"f{i}") for i in range(6)]
            sqb = sb([128, TB], BF16, "sqb")
            lbv = sb([128, 8], F32, "lbv")
            I("dve", "tensor_tensor", [self.pv], [lbv], out=lbv[:, 0:4], in0=self.pvs("lb1"), in1=self.pvs("lb0"), op=ALU.subtract)
            I("act", "activation", [lbv], [lbv], out=lbv[:, 0:4], in_=lbv[:, 0:4], func=AF.Sigmoid)
            I("dve", "tensor_scalar", [lbv], [lbv], out=lbv[:, 4:8], in0=lbv[:, 0:4], scalar1=-1.0, scalar2=1.0, op0=ALU.mult, op1=ALU.add)
            gdp = sb([128, 8], F32, "gdp")
            P.dma("sp", gdp[:], self.gdp_d.partition_broadcast(128), writes=[gdp])
            I("act", "activation", [gdp], [gdp], out=gdp[:, 0:4], in_=gdp[:, 0:4], func=AF.Exp)
            I("dve", "tensor_scalar", [gdp], [gdp], out=gdp[:, 0:4], in0=gdp[:, 0:4], scalar1=-1.0, scalar2=None, op0=ALU.mult)
            wabb = sb([128, 8, 8], BF16, "wabb")
            P.dma("pool", wabb[:], self.wab.rearrange("p (k n) -> p k n", k=8), writes=[wabb])
            rowm = sb([128, 4], F32, "rowm")
            for r in range(4):
                I("dve", "tensor_copy", [self.cst], [rowm], out=rowm[:, r:r + 1], in_=self.C("hmaskT")[:, 32 * r + 31:32 * r + 32])
            hS32 = [[sb([128, 128], F32, "hS32") for _ in range(2)] for _ in range(4)]
            hSb0 = [sb([128, 128], BF16, "hSb0") for _ in range(4)]
            for h in range(4):
                I("pool", "memset", [], [hS32[h][0]], ap=hS32[h][0][:], constant=0.0)
                I("pool", "memset", [], [hSb0[h]], ap=hSb0[h][:], constant=0.0)
            sgh = sb([128, TB], BF16, "sgh")
            gS32 = [[sb([128, 128], F32, "gS32") for _ in range(2)] for _ in range(4)]
            gSb = [[sb([128, 128], BF16, "gSb") for _ in range(2)] for _ in range(4)]
            for h in range(4):
                I("pool", "memset", [], [gS32[h][0]], ap=gS32[h][0][:], constant=0.0)
                I("pool", "memset", [], [gSb[h][0]], ap=gSb[h][0][:], constant=0.0)
            carry = [[sb([128, 3], F32, "carry") for _ in range(3)] for _ in range(4)]
            for h in range(4):
                for w_ in range(3):
                    I("pool", "memset", [], [carry[h][w_]], ap=carry[h][w_][:], constant=0.0)
            GTb = sb([128, 4, 40], F32, "GT")
            GT = GTb[:]
            gt = [P.view(GTb[:, i, :], "gtv", parent=GTb, share=True) for i in range(4)]
            shb = [sb([128, TB], BF16, f"shb{i}") for i in range(19)]
            Sring = [P.view(shb[12 + n // 4][:, (n % 4) * 128:(n % 4 + 1) * 128], "Sring", parent=shb[12 + n // 4], share=True) for n in range(17)]
            dlast = sb([128, 16], F32, "dlast")
            pt = [sb([128, 128], BF16, "pt") for _ in range(2)]
            work = sb([128, TB + 3], F32, "work")
            egT = sb([128, TB], F32, "egT")
            s32q = [sb([128, TB], F32, f"s32q{i}") for i in range(6)]
            u324 = sb([128, TB], F32, "u324")
            vnb = [sb([128, 128], BF16, "vnb") for _ in range(2)]
            hcount = [0] * 4
            gcount = [0] * 4
            nmf = lambda tb_: self.norm_mod(nm, tb_, lambda k: self.amv(1, s, 0, k), lambda k: self.modv(1, s, 0, k), hTs[tb_ % 2])
            nmf(0)
            for tb in range(NB):
                hT = hTs[tb % 2]
                qi, ki_, qbx, kl = shb[0:4]
                kt4 = shb[4:8]
                vtm = shb[8:12]
                whq = self.wload("w1i", 0)
                whf = self.wload("w1i", 1)
                whi = self.wload("w1i", 2)
                for i in range(4):
                    psv = self.big()
                    self.proj_tm(whi, hT, i, psv)
                    I("act", "activation", [psv], [vtm[i]], out=vtm[i][:], in_=psv[:], func=AF.Identity)
                whg = self.wload("w1i", 3)
                for h in range(4):
                    sgm, lf, kk, b, d1, e1 = F
                    o32 = sgm
                    psf = self.big()
                    self.proj_fm(whf, hT, h, psf)
                    I("act", "activation", [psf], [sgm], out=sgm[:], in_=psf[:], func=AF.Sigmoid)
                    psg = self.big()
                    self.proj_fm(whg, hT, h, psg)
                    I("act", "activation", [psg], [sgh], out=sgh[:], in_=psg[:], func=AF.Silu)
                    I("act", "activation", [sgm, lbv], [sgm], out=sgm[:], in_=sgm[:], func=AF.Identity, scale=lbv[:, 4 + h:5 + h], bias=lbv[:, h:h + 1])
                    I("act", "activation", [sgm], [lf], out=lf[:], in_=sgm[:], func=AF.Ln)
                    I("act", "activation", [sgm], [kk], out=kk[:], in_=sgm[:], func=AF.Identity, scale=-1.0, bias=1.0)
                    I("dve", "tensor_tensor_scan", [lf, self.cst], [b], out=b[:], data0=self.C("hrst"), data1=lf[:], initial=0.0, op0=ALU.mult, op1=ALU.add)
                    b3 = b[:].rearrange("p (n c) -> p n c", c=32)
                    v3 = lambda t: t[:].rearrange("p (n c) -> p n c", c=32)
                    I("dve", "tensor_tensor", [b], [d1], out=v3(d1), in0=b3, in1=b3[:, :, 16:17].to_broadcast([128, 16, 32]), op=ALU.subtract)
                    psq = self.big()
                    self.proj_fm(whq, hT, h, psq)
                    I("act", "activation", [d1], [e1], out=e1[:], in_=d1[:], func=AF.Exp)
                    I("dve", "tensor_tensor", [psq, e1], [qi], out=qi[:], in0=psq[:], in1=e1[:], op=ALU.mult)
                    I("act", "activation", [d1], [e1], out=e1[:], in_=d1[:], func=AF.Exp, scale=-1.0)
                    I("dve", "tensor_tensor", [kk, e1], [ki_], out=ki_[:], in0=kk[:], in1=e1[:], op=ALU.mult)
                    I("act", "activation", [b], [e1], out=e1[:], in_=b[:], func=AF.Exp)
                    I("dve", "tensor_tensor", [psq, e1], [qbx], out=qbx[:], in0=psq[:], in1=e1[:], op=ALU.mult)
                    I("dve", "tensor_tensor", [b], [d1], out=v3(d1), in0=b3[:, :, 31:32].to_broadcast([128, 16, 32]), in1=b3, op=ALU.subtract)
                    I("act", "activation", [d1], [e1], out=e1[:], in_=d1[:], func=AF.Exp)
                    I("dve", "tensor_tensor", [kk, e1], [kl], out=kl[:], in0=kk[:], in1=e1[:], op=ALU.mult)
                    I("act", "activation", [b], [dlast], out=dlast[:].unsqueeze(2), in_=b3[:, :, 31:32], func=AF.Exp)
                    tkb = self.tps().root
                    for i in range(4):
                        I("pe", "transpose", [kl, self.cstb], [tkb], inc=(i == 3), out=tkb[:, i * 128:(i + 1) * 128], in_=kl[:, i * 128:(i + 1) * 128], identity=self.identb)
                    for r in range(4):
                        if r % 2 == 0:
                            I("act", "activation", [tkb, rowm], [kt4[r]], out=kt4[r][:], in_=tkb[:, 0:512], func=AF.Identity, scale=rowm[:, r:r + 1])
                        else:
                            I("dve", "tensor_scalar", [tkb, rowm], [kt4[r]], out=kt4[r][:], in0=tkb[:, 0:512], scalar1=rowm[:, r:r + 1], scalar2=None, op0=ALU.mult)
                    I("pool", "tensor_copy", [hSb0[h]], [Sring[0]], out=Sring[0][:], in_=hSb0[h][:])
                    for n in range(16):
                        i, r = n // 4, n % 4
                        kvp = self.small()
                        self.mm(kvp, kvp[:], kt4[r][:, i * 128:(i + 1) * 128], vtm[i][:, h * 128:(h + 1) * 128], [kt4[r], vtm[i]])
                        gi = hcount[h]
                        hcount[h] += 1
                        so, sn = hS32[h][gi % 2], hS32[h][(gi + 1) % 2]
                        I("dve", "scalar_tensor_tensor", [so, kvp, dlast], [sn], out=sn[:], in0=so[:], scalar=dlast[:, n:n + 1], in1=kvp[:], op0=ALU.mult, op1=ALU.add)
                        I("dve", "scalar_tensor_tensor", [so, kvp, dlast], [Sring[n + 1]], out=Sring[n + 1][:], in0=so[:], scalar=dlast[:, n:n + 1], in1=kvp[:], op0=ALU.mult, op1=ALU.add)
                    I("pool", "tensor_copy", [Sring[16]], [hSb0[h]], out=hSb0[h][:], in_=Sring[16][:])
                    ob = self.big()
                    for i in range(4):
                        sc = self.small()
                        self.mm(sc, sc[:], ki_[:, i * 128:(i + 1) * 128], qi[:, i * 128:(i + 1) * 128], [ki_, qi])
                        p_ = pt[i % 2]
                        I("dve", "tensor_tensor", [sc, self.cst], [p_], out=p_[:], in0=sc[:], in1=self.C("hmaskT"), op=ALU.mult)
                        for r in range(4):
                            n = i * 4 + r
                            self.mm(ob, ob[:, n * 32:(n + 1) * 32], vtm[i][:, h * 128:(h + 1) * 128], p_[:, r * 32:(r + 1) * 32], [vtm[i], p_], start=True, stop=False, inc=False)
                            self.mm(ob, ob[:, n * 32:(n + 1) * 32], Sring[n][:], qbx[:, n * 32:(n + 1) * 32], [Sring[n], qbx], start=False, stop=True)
                    I("act", "activation", [ob], [o32], out=o32[:], in_=ob[:], func=AF.Identity)
                    self.rstd_bc(o32, sqb, d1, self.onesdivb)
                    I("dve", "tensor_tensor", [o32, d1], [o32], out=o32[:], in0=o32[:], in1=d1[:], op=ALU.mult)
                    I("dve", "scalar_tensor_tensor", [o32, sgh, self.pv], [mix[h]], out=mix[h][:], in0=o32[:], scalar=self.pvs("hgw", 0), in1=sgh[:], op0=ALU.mult, op1=ALU.mult)
                qT, kT, vT, qgT = shb[0:4]
                b16q = shb[4:14]
                wTnA4, wTnB4, kgA4, kgB4, qkT4 = shb[14:19]
                for i in range(4):
                    g_ = gt[i]
                    pab = self.small()
                    for k in range(8):
                        self.mm(pab, pab[:, 0:8], hT[k][:, i * 128:(i + 1) * 128], wabb[:, k, :], [hT[k], wabb], start=(k == 0), stop=(k == 7))
                    I("dve", "tensor_tensor", [pab, gdp], [g_], out=g_[:, 0:4], in0=pab[:, 0:4], in1=gdp[:, 4:8], op=ALU.add)
                    I("act", "activation", [pab], [g_], out=g_[:, 4:8], in_=pab[:, 4:8], func=AF.Sigmoid)
                for i in range(4):
                    g_ = gt[i]
                    I("act", "activation", [g_], [g_], out=g_[:, 0:4], in_=g_[:, 0:4], func=AF.Exp)
                    I("act", "activation", [g_], [g_], out=g_[:, 0:4], in_=g_[:, 0:4], func=AF.Ln, bias=1.0)
                    I("dve", "tensor_tensor", [g_, gdp], [g_], out=g_[:, 0:4], in0=g_[:, 0:4], in1=gdp[:, 0:4], op=ALU.mult)
                    pg = self.small()
                    for q_, nm_ in enumerate(("gU", "gsame", "gselA", "gselB")):
                        self.mm(pg, pg[:, q_ * 4:(q_ + 1) * 4], self.C(nm_), g_[:, 0:4], [g_, self.cst])
                    I("act", "activation", [pg], [g_], out=g_[:, 8:12], in_=pg[:, 0:4], func=AF.Exp)
                    I("dve", "tensor_copy", [pg], [g_], out=g_[:, 32:40], in_=pg[:, 0:8])
                    I("dve", "tensor_tensor", [g_], [g_], out=g_[:, 12:16], in0=g_[:, 36:40], in1=g_[:, 32:36], op=ALU.subtract)
                    I("act", "activation", [g_], [g_], out=g_[:, 12:16], in_=g_[:, 12:16], func=AF.Exp)
                    I("act", "activation", [pg], [g_], out=g_[:, 20:28], in_=pg[:, 8:16], func=AF.Exp)
                    I("dve", "tensor_scalar", [g_, self.cst], [g_], out=g_[:, 16:20], in0=g_[:, 12:16], scalar1=self.C("gselB")[:, 0:1], scalar2=None, op0=ALU.mult)
                    I("dve", "tensor_scalar", [g_, self.cst], [g_], out=g_[:, 12:16], in0=g_[:, 12:16], scalar1=self.C("gselA")[:, 0:1], scalar2=None, op0=ALU.mult)
                    I("dve", "tensor_tensor", [g_], [g_], out=g_[:, 28:32], in0=g_[:, 4:8], in1=g_[:, 8:12], op=ALU.mult)
                wdq = self.wload("w1i", 4)
                wdk = self.wload("w1i", 5)
                wdv = self.wload("w1i", 6)
                wdz = self.wload("w1i", 7)
                for h in range(4):
                    qc, kc, vc, rn, o32 = F[:5]
                    psq = self.big()
                    self.proj_fm(wdq, hT, h, psq)
                    self.conv_silu(psq, work, carry[h][0], h, qc)
                    psk = self.big()
                    self.proj_fm(wdk, hT, h, psk)
                    self.conv_silu(psk, work, carry[h][1], 4 + h, kc)
                    psv = self.big()
                    self.proj_fm(wdv, hT, h, psv)
                    self.conv_silu(psv, work, carry[h][2], 8 + h, vc)
                    psz = self.big()
                    self.proj_fm(wdz, hT, h, psz)
                    I("act", "activation", [psz], [sgh], out=sgh[:], in_=psz[:], func=AF.Silu)
                    I("act", "activation", [vc], [vT], out=vT[:], in_=vc[:], func=AF.Identity)
                    self.rstd_bc(kc, sqb, rn, self.onesb)
                    I("dve", "tensor_tensor", [kc, rn], [kT], out=kT[:], in0=kc[:], in1=rn[:], op=ALU.mult)
                    self.rstd_bc(qc, sqb, rn, self.onesb)
                    I("dve", "scalar_tensor_tensor", [qc, rn], [qc], out=qc[:], in0=qc[:], scalar=128.0 ** -0.5, in1=rn[:], op0=ALU.mult, op1=ALU.mult)
                    I("act", "activation", [qc], [qT], out=qT[:], in_=qc[:], func=AF.Identity)
                    TS = [slice(i * 128, (i + 1) * 128) for i in range(4)]
                    v4 = lambda b_: b_[:].rearrange("p (a b) -> p a b", a=4)
                    bc = lambda ap2: ap2.unsqueeze(1).to_broadcast([128, 4, 128])
                    sc4 = lambda col: GT[:, :, col:col + 1].to_broadcast([128, 4, 128])
                    G4, E4, ET4, Ds4, DTi4, G24 = s32q
                    A4, B4, A24, B24, IA4, P04, P14, kbg4, vb4, XT4 = b16q
                    Af4 = G4

                    def mm4(lhs, rhs, R):
                        bank = self.big()
                        for i in range(4):
                            self.mm(bank, bank[:, TS[i]], lhs(i), rhs(i), R, inc=(i == 3))
                        return bank

                    I("dve", "tensor_tensor", [self.cst, GTb], [G4], out=v4(G4), in0=bc(self.C("gMgt")), in1=sc4(h), op=ALU.mult)
                    I("pool", "tensor_tensor", [self.cst, GTb], [G24], out=v4(G24), in0=bc(self.C("ones")), in1=sc4(h), op=ALU.mult)
                    gU = self.C("gU")
                    pd4 = mm4(lambda i: gU, lambda i: G4[:, TS[i]], [G4, self.cst])
                    I("act", "activation", [pd4], [E4], out=E4[:], in_=pd4[:], func=AF.Exp)
                    pdt4 = mm4(lambda i: G4[:, TS[i]], lambda i: gU, [G4, self.cst])
                    I("act", "activation", [pdt4], [ET4], out=ET4[:], in_=pdt4[:], func=AF.Exp)
                    pgb4 = mm4(lambda i: G24[:, TS[i]], lambda i: gU, [G24, self.cst])
                    I("act", "activation", [pgb4], [egT], out=egT[:], in_=pgb4[:], func=AF.Exp)
                    I("dve", "tensor_tensor", [E4, self.cst], [Ds4], out=v4(Ds4), in0=v4(E4), in1=bc(self.C("gMstrict")), op=ALU.mult)
                    I("dve", "tensor_tensor", [Ds4, GTb], [Ds4], out=v4(Ds4), in0=v4(Ds4), in1=sc4(4 + h), op=ALU.mult)
                    I("pool", "tensor_tensor", [ET4, self.cst], [DTi4], out=v4(DTi4), in0=v4(ET4), in1=bc(self.C("gMinclT")), op=ALU.mult)
                    pkk4 = mm4(lambda i: kT[:, TS[i]], lambda i: kT[:, TS[i]], [kT])
                    I("dve", "tensor_tensor", [pkk4, Ds4], [Af4], out=Af4[:], in0=pkk4[:], in1=Ds4[:], op=ALU.mult)
                    I("pool", "tensor_copy", [Af4], [A4], out=A4[:], in_=Af4[:])
                    pqk4 = mm4(lambda i: kT[:, TS[i]], lambda i: qT[:, TS[i]], [kT, qT])
                    I("dve", "tensor_tensor", [pqk4, DTi4], [qkT4], out=qkT4[:], in0=pqk4[:], in1=DTi4[:], op=ALU.mult)
                    pbt4 = self.big()
                    for i in range(4):
                        I("pe", "transpose", [Af4, self.cst], [pbt4], inc=(i == 3), out=pbt4[:, TS[i]], in_=Af4[:, TS[i]], identity=self.identf)
                    I("act", "activation", [pbt4], [B4], out=B4[:], in_=pbt4[:], func=AF.Identity)
                    I("dve", "scalar_tensor_tensor", [pbt4, self.cst], [P04], out=v4(P04), in0=v4(pbt4), scalar=-1.0, in1=bc(self.identf), op0=ALU.mult, op1=ALU.add)
                    Ap, Bp, An, Bn, Pc, Pn = A4, B4, A24, B24, P04, P14
                    for lv in range(5):
                        pa4 = mm4(lambda i: Bp[:, TS[i]], lambda i: Ap[:, TS[i]], [Bp, Ap])
                        if lv < 4:
                            I("act", "activation", [pa4], [An], out=An[:], in_=pa4[:], func=AF.Identity)
                        I("dve", "tensor_tensor", [pa4, self.cst], [IA4], out=v4(IA4), in0=v4(pa4), in1=bc(self.identf), op=ALU.add)
                        if lv < 4:
                            pb4 = mm4(lambda i: Ap[:, TS[i]], lambda i: Bp[:, TS[i]], [Ap, Bp])
                            I("act", "activation", [pb4], [Bn], out=Bn[:], in_=pb4[:], func=AF.Identity)
                        pp4 = mm4(lambda i: IA4[:, TS[i]], lambda i: Pc[:, TS[i]], [IA4, Pc])
                        dst = XT4 if lv == 4 else Pn
                        I("dve", "tensor_copy", [pp4], [dst], out=dst[:], in_=pp4[:])
                        Ap, An = An, Ap
                        Bp, Bn = Bn, Bp
                        Pc, Pn = Pn, Pc
                    tk = self.tps().root
                    for i in range(4):
                        I("pe", "transpose", [kT, self.cstb], [tk], inc=(i == 3), out=tk[:, TS[i]], in_=kT[:, TS[i]], identity=self.identb)
                    tk4 = tk[:, 0:512].rearrange("p (a b) -> p a b", a=4)
                    I("dve", "tensor_tensor", [tk, GTb], [kbg4], out=v4(kbg4), in0=tk4, in1=sc4(28 + h), op=ALU.mult)
                    I("dve", "tensor_tensor", [tk, GTb], [kgA4], out=v4(kgA4), in0=tk4, in1=sc4(12 + h), op=ALU.mult)
                    I("dve", "tensor_tensor", [tk, GTb], [kgB4], out=v4(kgB4), in0=tk4, in1=sc4(16 + h), op=ALU.mult)
                    tv = self.tps().root
                    for i in range(4):
                        I("pe", "transpose", [vT, self.cstb], [tv], inc=(i == 3), out=tv[:, TS[i]], in_=vT[:, TS[i]], identity=self.identb)
                    I("dve", "tensor_tensor", [tv, GTb], [vb4], out=v4(vb4), in0=tv[:, 0:512].rearrange("p (a b) -> p a b", a=4), in1=sc4(4 + h), op=ALU.mult)
                    pw4 = mm4(lambda i: kbg4[:, TS[i]], lambda i: XT4[:, TS[i]], [kbg4, XT4])
                    I("dve", "scalar_tensor_tensor", [pw4, self.cst], [wTnA4], out=v4(wTnA4), in0=v4(pw4), scalar=-1.0, in1=bc(self.colA), op0=ALU.mult, op1=ALU.mult)
                    I("dve", "scalar_tensor_tensor", [pw4, self.cst], [wTnB4], out=v4(wTnB4), in0=v4(pw4), scalar=-1.0, in1=bc(self.colB), op0=ALU.mult, op1=ALU.mult)
                    pu4 = mm4(lambda i: XT4[:, TS[i]], lambda i: vb4[:, TS[i]], [XT4, vb4])
                    I("act", "activation", [pu4], [u324], out=u324[:], in_=pu4[:], func=AF.Identity)
                    I("dve", "tensor_tensor", [qc, egT], [qgT], out=qgT[:], in0=qc[:], in1=egT[:], op=ALU.mult)
                    ob = self.big()
                    for n in range(8):
                        i, half = n // 2, n % 2
                        g_ = gt[i]
                        gi = gcount[h]
                        gcount[h] += 1
                        so, sn = gS32[h][gi % 2], gS32[h][(gi + 1) % 2]
                        bo, bn = gSb[h][gi % 2], gSb[h][(gi + 1) % 2]
                        wT4 = wTnA4 if half == 0 else wTnB4
                        kg4 = kgA4 if half == 0 else kgB4
                        ti = slice(i * 128, (i + 1) * 128)
                        lastc = 20 + 4 * half + h
                        pv_ = self.small()
                        self.mm(pv_, pv_[:], wT4[:, ti], bo[:], [wT4, bo])
                        vn = vnb[n % 2]
                        I("dve", "tensor_tensor", [pv_, u324], [vn], out=vn[:], in0=pv_[:], in1=u324[:, ti], op=ALU.add)
                        cs = slice(n * 64, (n + 1) * 64)
                        self.mm(ob, ob[:, cs], bo[:], qgT[:, cs], [bo, qgT], start=True, stop=False, inc=False)
                        self.mm(ob, ob[:, cs], vn[:], qkT4[:, i * 128 + half * 64:i * 128 + (half + 1) * 64], [vn, qkT4], start=False, stop=True)
                        pkv = self.small()
                        self.mm(pkv, pkv[:], kg4[:, ti], vn[:], [kg4, vn])
                        I("dve", "scalar_tensor_tensor", [so, pkv, g_], [bn], out=bn[:], in0=so[:], scalar=g_[:, lastc:lastc + 1], in1=pkv[:], op0=ALU.mult, op1=ALU.add)
                        I("dve", "scalar_tensor_tensor", [so, pkv, g_], [sn], out=sn[:], in0=so[:], scalar=g_[:, lastc:lastc + 1], in1=pkv[:], op0=ALU.mult, op1=ALU.add)
                    I("act", "activation", [ob], [o32], out=o32[:], in_=ob[:], func=AF.Identity)
                    self.rstd_bc(o32, sqb, rn, self.onesdivb)
                    I("dve", "tensor_tensor", [o32, rn], [o32], out=o32[:], in0=o32[:], in1=rn[:], op=ALU.mult)
                    I("dve", "scalar_tensor_tensor", [o32, sgh, self.pv], [mix[4 + h]], out=mix[4 + h][:], in0=o32[:], scalar=self.pvs("gdw", 0), in1=sgh[:], op0=ALU.mult, op1=ALU.mult)
                off = os.environ.get("M1OFF", "")
                for c in range(8):
                    if (off == "hg" and c < 4) or (off == "gd" and c >= 4):
                        I("pool", "memset", [], [mix[c]], ap=mix[c][:], constant=0.0)
                if tb + 1 < NB:
                    nmf(tb + 1)
                self.out_proj("wo1", mix, 1, s, tb)

    def mlp(self, l, s):
        P = self.P
        with P.scope():
            nm = self.nm_alloc()
            nm["ps"] = P.ps([128, TB], F32, "nps")
            hTs = [[P.sb([128, TB], BF16, "hT") for _ in range(8)] for _ in range(2)]
            hid = [P.sb([128, TB], BF16, "hid") for _ in range(32)]
            r32 = [P.sb([128, TB], F32, "r32") for _ in range(2)]
            pA = [P.ps([128, TB], F32, "pA") for _ in range(2)]
            pB = [P.ps([128, TB], F32, "pB") for _ in range(4)]
            if l == 0 and s == 0:
                self.late_conv()
            nmf = lambda tb_: self.norm_mod(nm, tb_, lambda k: self.amv(l, s, 1, k), lambda k: self.modv(l, s, 3, k), hTs[tb_ % 2])
            nmf(0)
            for tb in range(NB):
                hT = hTs[tb % 2]
                for g in range(8):
                    w = self.wload(f"m1_{l}", g)
                    for j in range(4):
                        c = g * 4 + j
                        ps = pA[c % 2]
                        self.proj_fm(w, hT, j, ps)
                        r = r32[c % 2]
                        self.I("act", "activation", [ps], [r], out=r[:], in_=ps[:], func=AF.Relu)
                        self.I("dve", "tensor_tensor", [r], [hid[c]], out=hid[c][:], in0=r[:], in1=r[:], op=ALU.mult)
                if tb + 1 < NB:
                    nmf(tb + 1)
                for ng in range(2):
                    for kg in range(4):
                        wb, w3 = self.wload(f"m2_{l}", kg * 2 + ng)
                        for mi in range(4):
                            for c in range(8):
                                self.mm(pB[mi], pB[mi][:], w3[:, c, mi * 128:(mi + 1) * 128], hid[kg * 8 + c][:], [wb, hid[kg * 8 + c]],
                                        start=(kg == 0 and c == 0), stop=(kg == 3 and c == 7), inc=(c == 7))
                    for mi in range(4):
                        m = ng * 4 + mi
                        xb = self.xs[m][tb]
                        self.I("dve", "scalar_tensor_tensor", [pB[mi], xb], [xb], out=xb[:], in0=pB[mi][:],
                               scalar=self.modv(l, s, 5, m), in1=xb[:], op0=ALU.mult, op1=ALU.add)

    def final(self, s):
        P = self.P
        with P.scope():
            nm = self.nm_alloc()
            nm["ps"] = P.ps([128, TB], F32, "nps")
            ob = [P.sb([128, TB], F32, "ob") for _ in range(4)]
            for tb in range(NB):
                rs = self.norm_stats(nm, tb)
                for k in range(8):
                    o = ob[k % 4]
                    self.I("dve", "scalar_tensor_tensor", [self.xs[k][tb], rs], [o], out=o[:], in0=self.xs[k][tb][:],
                           scalar=self.pvs("fnw", k), in1=rs[:], op0=ALU.mult, op1=ALU.mult)
                    P.dma("sp", self.outT[s, k * 128:(k + 1) * 128, tb * TB:(tb + 1) * TB], o[:], reads=[o])

    def load_x(self, s):
        for k in range(8):
            self.P.dma("sp", self.xs_t[k][:], self.xT[s, k * 128:(k + 1) * 128, :], writes=self.xs[k])

    def build(self, stage=9):
        P = self.P
        with P.stack:
            self.setup()
            for s in range(self.nseq):
                if stage < 1:
                    break
                self.load_x(s)
                for l in range(self.nlayers):
                    if not self.skip_mixer:
                        (self.mixer0 if l == 0 else self.mixer1)(s)
                    if stage >= 2:
                        self.mlp(l, s)
                self.final(s)
            P.barrier(final=True)
            P.emit()
        return self.nc


def _prep_shared(inp):
    f = lambda a: np.ascontiguousarray(np.asarray(a, np.float32))
    w_in = f(inp["ev_w_in"][0])
    perm = np.concatenate([np.arange(h * 128 + 64, h * 128 + 128).tolist() + np.arange(h * 128, h * 128 + 64).tolist() for h in range(4)]).astype(np.int64)
    w0 = np.concatenate([w_in, w_in[:, 1024 + perm], w_in[:, 1536 + perm]], axis=1)
    od = f(inp["od_w_in"][0])
    wab = np.ascontiguousarray(od[:, 4096:4104].reshape(8, 128, 8).transpose(1, 0, 2).reshape(128, 64))
    wa = f(inp["lru_w_a"][0])
    wx = f(inp["lru_w_x"][0])
    wbd = np.zeros((128, 8, 128), np.float32)
    for c in range(4):
        for half in range(2):
            sl = slice(half * 64, half * 64 + 64)
            wbd[sl, c, sl] = wa[2 * c + half]
            wbd[sl, 4 + c, sl] = wx[2 * c + half]
    pk = _params(inp)
    shared = {
        "ada_w": f(inp["ada_w"]), "ada_b": f(inp["ada_b"]),
        "w0": np.ascontiguousarray(w0), "wo0": f(inp["ev_w_out"][0]),
        "w1i": np.ascontiguousarray(od[:, :4096]), "wo1": f(inp["od_w_out"][0]),
        "m1_0": f(inp["mlp_w1"][0]), "m2_0": f(inp["mlp_w2"][0]),
        "m1_1": f(inp["mlp_w1"][1]), "m2_1": f(inp["mlp_w2"][1]),
        "wab": wab, "pv": pk.array(), "cst": _CST.array(), "wbd": np.ascontiguousarray(wbd.reshape(128, 1024)),
        "gdp": np.ascontiguousarray(np.concatenate([f(inp["gd_a_log"][0]), f(inp["gd_dt_bias"][0])])[None, :]),
    }
    return shared, pk


def _core_inputs(inp, shared, seqs):
    x = np.asarray(inp["x"], np.float32)
    c = np.asarray(inp["c"], np.float32)
    pos = np.asarray(inp["positions"], np.int32)
    ns = len(seqs)
    m = dict(shared)
    m["xT"] = np.ascontiguousarray(np.stack([x[b].T for b in seqs]))
    cs = list(seqs) if len(seqs) > 1 else [seqs[0], seqs[0]]
    cc = np.stack([c[b] for b in cs], 1)
    ns = len(cs)
    m["cT"] = np.ascontiguousarray(cc.reshape(8, 128, ns).transpose(1, 0, 2).reshape(128, 8 * ns))
    m["pos"] = np.ascontiguousarray(np.stack([pos[b] for b in seqs]))
    return m


_NC_CACHE = {}


def kernel(**inputs):
    shared, pk = _prep_shared(inputs)
    ncores, nseq = 8, 2
    key = (nseq, 2)
    if key not in _NC_CACHE:
        _NC_CACHE[key] = Builder(pk.off, pk.n, nseq=nseq, nlayers=2).build()
    nc = _NC_CACHE[key]
    in_maps = [_core_inputs(inputs, shared, [i * nseq + j for j in range(nseq)]) for i in range(ncores)]
    res = run_bass_kernel_spmd(nc, in_maps, core_ids=list(range(ncores)))
    out = np.empty((16, T, 1024), np.float32)
    for i in range(ncores):
        o = res.results[i]["outT"]
        for j in range(nseq):
            out[i * nseq + j] = o[j].T
    return out
```

```python
import contextlib
import numpy as np
import concourse.bass as bass
import concourse.mybir as mybir
from concourse.bass_utils import run_bass_kernel_spmd

F32 = mybir.dt.float32
BF16 = mybir.dt.bfloat16
I32 = mybir.dt.int32
AF = mybir.ActivationFunctionType
ALU = mybir.AluOpType

T = 2048
TB = 512
NB = T // TB
EPS = 1e-6


class Buf:
    __slots__ = ("name", "lw", "rs", "ap", "root", "excl")

    def __init__(self, name, ap=None, root=None, excl=False):
        self.name = name
        self.lw = None
        self.rs = []
        self.ap = ap
        self.root = root if root is not None else self
        self.excl = excl

    def __getitem__(self, k):
        return self.ap[k]


class Prog:
    ENGS = ("pe", "dve", "act", "pool", "sp")
    NRING = 16

    def __init__(self, nc):
        self.nc = nc
        self.ops = {e: [] for e in self.ENGS}
        self.nmile = {e: 0 for e in self.ENGS}
        self.waited = {e: {} for e in self.ENGS}
        self.dma_n = {e: 0 for e in self.ENGS}
        self.dma_cnt = {}
        self.stack = contextlib.ExitStack()
        self.stacks = [self.stack]
        self.nalloc = 0
        self.sems = {}
        keys = [e for e in self.ENGS if e != "sp"]
        keys += [("q", e, i) for e in ("sp", "act", "pool") for i in range(self.NRING)]
        for k in keys:
            nm = k if isinstance(k, str) else f"q_{k[1]}_{k[2]}"
            self.sems[k] = self.stack.enter_context(nc.semaphore("s_" + nm))

    @contextlib.contextmanager
    def scope(self):
        st = contextlib.ExitStack()
        self.stacks.append(st)
        with st:
            yield
            self.barrier()
            self.emit()
        self.stacks.pop()

    def sb(self, shape, dtype, name=None):
        self.nalloc += 1
        name = f"{name or 't'}_{self.nalloc}"
        t = self.stacks[-1].enter_context(self.nc.sbuf_tensor(name, list(shape), dtype))
        return Buf(name, t)

    def ps(self, shape, dtype=F32, name=None):
        self.nalloc += 1
        name = f"{name or 'p'}_{self.nalloc}"
        t = self.stacks[-1].enter_context(self.nc.psum_tensor(name, list(shape), dtype))
        return Buf(name, t, excl=True)

    def view(self, ap, name="v", parent=None, share=False):
        if parent is not None and (parent.excl or share):
            return Buf(name, ap, root=parent.root, excl=parent.excl)
        return Buf(name, ap)

    @staticmethod
    def _split(reads, writes):
        r2, w2 = [], []
        for b in reads:
            (w2 if b.excl else r2).append(b.root)
        for b in writes:
            w2.append(b.root)
        return r2, w2

    def _deps(self, eng, reads, writes):
        deps = {}

        def add(tok, kind):
            if tok is None:
                return
            key, val = tok
            if key == eng:
                if kind != "raw" or eng == "pe":
                    return
            if deps.get(key, 0) < val:
                deps[key] = val

        for b in reads:
            add(b.lw, "raw")
        for b in writes:
            add(b.lw, "waw")
            for r in b.rs:
                add(r, "war")
        w = self.waited[eng]
        out = []
        for key, val in deps.items():
            if isinstance(key, str):
                assert val <= self.nmile[key], f"wait on future milestone {key} {val} > {self.nmile[key]} (eng {eng})"
            if w.get(key, 0) >= val:
                continue
            w[key] = val
            out.append((key, val))
        return out

    def op(self, eng, meth, kw, reads=(), writes=(), inc=True):
        reads, writes = self._split(reads, writes)
        waits = self._deps(eng, reads, writes)
        if inc:
            self.nmile[eng] += 1
            tok = (eng, self.nmile[eng])
        else:
            tok = (eng, self.nmile[eng] + 1)
        for b in reads:
            b.rs.append(tok)
        for b in writes:
            b.lw = tok
            b.rs = []
        self.ops[eng].append((meth, kw, waits, inc, None))
        return tok

    def dma(self, eng, out, in_, reads=(), writes=(), **kw):
        reads, writes = self._split(reads, writes)
        waits = self._deps(eng, reads, writes)
        i = self.dma_n[eng]
        self.dma_n[eng] += 1
        key = ("q", eng, i % self.NRING)
        prev = self.dma_cnt.get(key, 0)
        if prev and self.waited[eng].get(key, 0) < prev:
            self.waited[eng][key] = prev
            waits.append((key, prev))
        self.dma_cnt[key] = prev + 16
        tok = (key, self.dma_cnt[key])
        for b in reads:
            b.rs.append(tok)
        for b in writes:
            b.lw = tok
            b.rs = []
        kw = dict(kw)
        kw["out"] = out
        kw["in_"] = in_
        self.ops[eng].append(("dma_start", kw, waits, False, key))
        return tok

    def wait_all(self, eng, toks):
        waits = []
        w = self.waited[eng]
        for key, val in toks:
            if w.get(key, 0) < val:
                w[key] = val
                waits.append((key, val))
        self.ops[eng].append((None, None, waits, False, None))

    def barrier(self, final=False):
        toks = [(e, self.nmile[e]) for e in self.ENGS if self.nmile[e] > 0]
        toks += [(k, v) for k, v in self.dma_cnt.items() if final or k[1] != "pool"]
        for e in self.ENGS:
            self.wait_all(e, [t for t in toks if t[0] != e])

    def emit(self):
        nc = self.nc
        sems = self.sems
        ops = self.ops
        self.ops = {e: [] for e in self.ENGS}
        with nc.Block() as block:

            def run(eng_name, e):
                for meth, kw, waits, inc, dkey in ops[eng_name]:
                    for key, val in waits:
                        e.wait_ge(sems[key], val)
                    if meth is None:
                        continue
                    inst = getattr(e, meth)(**kw)
                    if dkey is not None:
                        inst.then_inc(sems[dkey], 16)
                    elif inc:
                        inst.then_inc(sems[eng_name], 1)

            @block.tensor
            def _(e):
                run("pe", e)

            @block.vector
            def _(e):
                run("dve", e)

            @block.scalar
            def _(e):
                run("act", e)

            @block.gpsimd
            def _(e):
                run("pool", e)

            @block.sync
            def _(e):
                run("sp", e)


def _fm(v):
    v = np.asarray(v, np.float32)
    return np.ascontiguousarray(v.reshape(-1, 128).T)


class _Pack:
    def __init__(self):
        self.cols = []
        self.off = {}
        self.n = 0

    def add(self, name, arr):
        arr = np.asarray(arr, np.float32)
        assert arr.shape[0] == 128
        arr = arr.reshape(128, -1)
        self.off[name] = (self.n, arr.shape[1])
        self.cols.append(arr)
        self.n += arr.shape[1]

    def array(self):
        return np.ascontiguousarray(np.concatenate(self.cols, axis=1))


def _consts():
    pk = _Pack()
    i = np.arange(128)
    pk.add("ident", np.eye(128))
    pk.add("onesdiv", np.full((128, 128), 1.0 / 128))
    pk.add("ones", np.ones((128, 128)))
    lg = np.log1p(-np.power(2.0, -5.0 - np.arange(4)))
    rel = (i[None, :] - i[:, None]).astype(np.float64)
    for h in range(4):
        pk.add(f"rmask{h}", np.where(rel >= 0, np.exp(lg[h] * np.maximum(rel, 0)), 0.0))
    for h in range(4):
        pk.add(f"qdec{h}", np.broadcast_to(np.exp(lg[h] * (i + 1.0))[None, :], (128, 128)))
    pk.add("kdec", np.stack([np.exp(lg[h] * (127.0 - i)) for h in range(4)], 1))
    half = 64
    inv = 10000.0 ** (-(np.arange(half, dtype=np.float32)) / half)
    pk.add("invf", np.concatenate([inv, inv]).astype(np.float32)[:, None])
    pk.add("sgn", np.concatenate([-np.ones(64), np.ones(64)])[:, None])
    t = np.arange(512)
    pk.add("hrst", np.broadcast_to((t % 32 != 0).astype(np.float32)[None, :], (128, 512)))
    same32 = (i[:, None] // 32) == (i[None, :] // 32)
    pk.add("hmaskT", (same32 & (i[:, None] <= i[None, :])).astype(np.float32))
    same = (i[:, None] // 64) == (i[None, :] // 64)
    pk.add("gU", (same & (i[:, None] <= i[None, :])).astype(np.float32))
    pk.add("gMgt", (same & (i[:, None] > i[None, :])).astype(np.float32))
    pk.add("gMincl", (same & (i[None, :] <= i[:, None])).astype(np.float32))
    pk.add("gMinclT", (same & (i[None, :] >= i[:, None])).astype(np.float32))
    pk.add("gMstrict", (same & (i[None, :] < i[:, None])).astype(np.float32))
    pk.add("gselA", np.broadcast_to((i < 64).astype(np.float32)[:, None], (128, 128)))
    pk.add("gselB", np.broadcast_to((i >= 64).astype(np.float32)[:, None], (128, 128)))
    pk.add("gsame", same.astype(np.float32))
    pk.add("gcolA", np.broadcast_to((i < 64).astype(np.float32)[None, :], (128, 128)))
    pk.add("gcolB", np.broadcast_to((i >= 64).astype(np.float32)[None, :], (128, 128)))
    return pk


_CST = _consts()
LRET = [float(np.log1p(-2.0 ** (-5.0 - h))) for h in range(4)]


def _params(inp):
    pk = _Pack()
    for l in range(2):
        pk.add(f"nmw{l}", _fm(inp["norm_mix_w"][l]))
        pk.add(f"nfw{l}", _fm(inp["norm_mlp_w"][l]))
    pk.add("fnw", _fm(inp["final_norm_w"]))
    for j in range(4):
        pk.add(f"cw{j}", _fm(inp["lru_conv_w"][0][j]))
    pk.add("cb", _fm(inp["lru_conv_b"][0]))
    pk.add("ba", _fm(inp["lru_b_a"][0]))
    pk.add("bx", _fm(inp["lru_b_x"][0]))
    pk.add("lam", _fm(inp["lru_lambda"][0]))
    pk.add("lb0", _fm(inp["hg_lb_logits"][0]))
    pk.add("lb1", _fm(inp["hg_lb_logits"][1]))
    for j in range(4):
        pk.add(f"gcw{j}", _fm(inp["gd_conv_w"][0][j]))
    pk.add("hgw", _fm(inp["hg_norm_w"][0]))
    pk.add("gdw", _fm(inp["gd_norm_w"][0]))
    return pk


import os
CUT = int(os.environ.get('CUT', '9'))


class Builder:
    def __init__(self, poff, npv, nseq=2, nlayers=2, skip_mixer=False):
        self.poff = poff
        self.nseq = nseq
        self.nlayers = nlayers
        self.skip_mixer = skip_mixer
        nc = self.nc = bass.Bass("TRN2", target_bir_lowering=False)
        self.P = Prog(nc)
        din = lambda n, s, d=F32: nc.dram_tensor(n, list(s), d, kind="ExternalInput").ap()
        self.xT = din("xT", [nseq, 1024, T])
        self.nsm = max(nseq, 2)
        self.cT = din("cT", [128, 8 * self.nsm])
        self.pos = din("pos", [nseq, T], I32)
        self.ada_w = din("ada_w", [2, 1024, 6144])
        self.ada_b = din("ada_b", [2, 6144])
        self.d_w = {
            "w0": din("w0", [1024, 4096]), "wo0": din("wo0", [1024, 1024]),
            "w1i": din("w1i", [1024, 4096]), "wo1": din("wo1", [1024, 1024]),
            "m1_0": din("m1_0", [1024, 4096]), "m2_0": din("m2_0", [4096, 1024]),
            "m1_1": din("m1_1", [1024, 4096]), "m2_1": din("m2_1", [4096, 1024]),
        }
        self.wab = din("wab", [128, 64])
        self.pv_d = din("pv", [128, npv])
        self.cst_d = din("cst", [128, _CST.n])
        self.wbd_d = din("wbd", [128, 8 * 128])
        self.gdp_d = din("gdp", [1, 8])
        self.outT = nc.dram_tensor("outT", [nseq, 1024, T], F32, kind="ExternalOutput").ap()
        self.npv = npv

    def I(self, eng, meth, R, W, inc=True, **kw):
        return self.P.op(eng, meth, kw, R, W, inc)

    def C(self, name):
        o, n = _CST.off[name]
        return self.cst[:, o:o + n]

    def pvs(self, name, j=None):
        o, n = self.poff[name]
        if j is None:
            return self.pv[:, o:o + n]
        return self.pv[:, o + j:o + j + 1]

    def mm(self, ps, out, lhsT, rhs, R, start=True, stop=True, inc=None):
        if inc is None:
            inc = stop
        return self.I("pe", "matmul", R, [ps], inc=inc, out=out, lhsT=lhsT, rhs=rhs, start=start, stop=stop)

    def big(self):
        b = self.pbig[self.ibig % len(self.pbig)]
        self.ibig += 1
        return b

    def small(self):
        b = self.psmall[self.ismall % len(self.psmall)]
        self.ismall += 1
        return b

    def tps(self):
        b = self.ptp[self.itp % len(self.ptp)]
        self.itp += 1
        return b

    def alloc_psum(self, nbig, nsmall_banks, ntp_banks):
        P = self.P
        self.pbig = [P.ps([128, TB], F32, "pbig") for _ in range(nbig)]
        self.ibig = 0
        banks = [P.ps([128, TB], F32, "psm") for _ in range(nsmall_banks)]
        self.psmall = [P.view(b[:, q * 128:(q + 1) * 128], "psmv", parent=b) for q in range(4) for b in banks]
        self.ismall = 0
        banks = [P.ps([128, 1024], BF16, "ptp") for _ in range(ntp_banks)]
        self.ptp = [P.view(b[:, q * 128:(q + 1) * 128], "ptpv", parent=b) for q in range(4) for b in banks]
        self.itp = 0

    def wload(self, name, ti):
        P = self.P
        buf = self.wring[self.wi % len(self.wring)]
        self.wi += 1
        scr, tb = self.wt[name]
        P.dma("sp", buf[:], scr[ti], reads=[tb[ti]], writes=[buf])
        return buf, buf[:].rearrange("p (k n) -> p k n", k=8)

    def proj_fm(self, w, hT, j, ps):
        wb, w3 = w
        for k in range(8):
            self.mm(ps, ps[:], w3[:, k, j * 128:(j + 1) * 128], hT[k][:], [wb, hT[k]], start=(k == 0), stop=(k == 7))

    def proj_tm(self, w, hT, i, ps):
        wb, w3 = w
        for k in range(8):
            self.mm(ps, ps[:], hT[k][:, i * 128:(i + 1) * 128], w3[:, k, :], [wb, hT[k]], start=(k == 0), stop=(k == 7))

    def nm_alloc(self):
        P = self.P
        return {
            "sq": [P.sb([128, TB], BF16, "nsq") for _ in range(2)],
            "rs": P.sb([128, TB], F32, "nrs"),
            "tmp": [P.sb([128, TB], F32, "ntmp") for _ in range(2)],
        }

    def norm_stats(self, nm, tb):
        ps = nm["ps"] if "ps" in nm else self.big()
        for k in range(8):
            sq = nm["sq"][k % 2]
            self.I("act", "activation", [self.xs[k][tb]], [sq], out=sq[:], in_=self.xs[k][tb][:], func=AF.Square)
            self.mm(ps, ps[:], self.onesb, sq[:], [sq, self.cstb], start=(k == 0), stop=(k == 7), inc=True)
        rs = nm["rs"]
        self.I("act", "activation", [ps], [rs], out=rs[:], in_=ps[:], func=AF.Ln, scale=1.0 / 1024, bias=self.epsc[:, 0:1])
        self.I("act", "activation", [rs], [rs], out=rs[:], in_=rs[:], func=AF.Exp, scale=-0.5)
        return rs

    def norm_mod(self, nm, tb, A, Bv, hT):
        rs = self.norm_stats(nm, tb)
        for k in range(8):
            tmp = nm["tmp"][k % 2]
            self.I("dve", "scalar_tensor_tensor", [self.xs[k][tb], rs], [tmp], out=tmp[:], in0=self.xs[k][tb][:],
                   scalar=A(k), in1=rs[:], op0=ALU.mult, op1=ALU.mult)
            self.I("act", "activation", [tmp, self.modT[0]], [hT[k]], out=hT[k][:], in_=tmp[:], func=AF.Identity, bias=Bv(k))

    def setup(self):
        P = self.P
        nseq, nl, nsm = self.nseq, self.nlayers, self.nsm
        self.cst = P.sb([128, _CST.n], F32, "cst")
        P.dma("sp", self.cst[:], self.cst_d, writes=[self.cst])
        self.cstb = P.sb([128, 384], BF16, "cstb")
        P.dma("pool", self.cstb[:], self.cst_d[:, 0:384], writes=[self.cstb])
        self.identb = self.cstb[:, 0:128]
        self.onesdivb = self.cstb[:, 128:256]
        self.onesb = self.cstb[:, 256:384]
        self.identf = self.C("ident")
        self.colA = self.C("gcolA")
        self.colB = self.C("gcolB")
        self.pv = P.sb([128, self.npv], F32, "pv")
        P.dma("sp", self.pv[:], self.pv_d, writes=[self.pv])
        self.epsc = P.sb([128, 1], F32, "epsc")
        self.I("dve", "memset", [], [self.epsc], ap=self.epsc[:], constant=EPS)
        self.onep = P.sb([128, 1], F32, "onep")
        self.I("dve", "memset", [], [self.onep], ap=self.onep[:], constant=1.0000001)
        self.wt = {}
        order = ["w0", "wo0", "m1_0", "m2_0", "w1i", "wo1", "m1_1", "m2_1"]
        if nl == 1:
            order = order[:4]

        def convert(name):
            src = self.d_w[name]
            K, N = src.shape
            kt, nt = K // 1024, N // 512
            scr = self.nc.dram_tensor(name + "_bf", [kt * nt, 128, 4096], BF16, kind="Internal").ap()
            srcv = src.rearrange("(kk p) n -> p kk n", p=128)
            tb = []
            for kg in range(kt):
                for ng in range(nt):
                    b = Buf(f"{name}_{kg}_{ng}")
                    P.dma("pool", scr[kg * nt + ng].rearrange("p (k n) -> p k n", k=8),
                          srcv[:, kg * 8:(kg + 1) * 8, ng * 512:(ng + 1) * 512], writes=[b])
                    tb.append(b)
            self.wt[name] = (scr, tb)

        for name in order[:4]:
            convert(name)
        self.late_conv = lambda: [convert(n) for n in order[4:]]
        self.wring = [P.sb([128, 4096], BF16, "wring") for _ in range(4)]
        self.wi = 0
        self.xs_t = [P.sb([128, T], F32, "xs") for _ in range(8)]
        self.xs = [[P.view(self.xs_t[k][:, tb * TB:(tb + 1) * TB], "xsv") for tb in range(NB)] for k in range(8)]
        self.modT = [P.sb([128, 48 * nsm], F32, "modT") for _ in range(nl)]
        self.AM = P.sb([128, nl * nseq * 16], F32, "AM")
        self.lruc = P.sb([128, 8], F32, "lruc")
        with P.scope():
            ca = P.sb([128, 8 * nsm], F32, "ca")
            P.dma("sp", ca[:], self.cT, writes=[ca])
            self.I("act", "activation", [ca], [ca], out=ca[:], in_=ca[:], func=AF.Silu)
            ca3 = ca[:].rearrange("p (k s) -> p k s", s=nsm)
            awr = [P.sb([128, 8, 512], F32, "awr") for _ in range(2)]
            mod2 = P.sb([nsm, 6144], F32, "mod2")
            bb = P.sb([nsm, 6144], F32, "bb")
            psr = [P.ps([128, TB], F32, "psr") for _ in range(2)]
            pst = P.ps([128, 48 * nsm], F32, "pst")
            for l in range(nl):
                P.dma("sp", bb[:], self.ada_b[l:l + 1, :].partition_broadcast(nsm), writes=[bb])
                awv = self.ada_w[l].rearrange("(k p) n -> p k n", p=128)
                for n in range(12):
                    aw = awr[n % 2]
                    P.dma("sp", aw[:], awv[:, :, n * 512:(n + 1) * 512], writes=[aw])
                    ps = psr[n % 2]
                    for k in range(8):
                        self.mm(ps, ps[0:nsm, :], ca3[:, k, :], aw[:, k, :], [ca, aw], start=(k == 0), stop=(k == 7))
                    self.I("dve", "tensor_tensor", [ps, bb], [mod2], out=mod2[:, n * 512:(n + 1) * 512], in0=ps[0:nsm, :],
                           in1=bb[:, n * 512:(n + 1) * 512], op=ALU.add)
                for j in range(48):
                    self.I("pe", "transpose", [mod2, self.cst], [pst], out=pst[:, j * nsm:(j + 1) * nsm],
                           in_=mod2[0:nsm, j * 128:(j + 1) * 128], identity=self.identf[0:nsm, 0:nsm])
                self.I("dve", "tensor_copy", [pst], [self.modT[l]], out=self.modT[l][:], in_=pst[:])
                mv = self.modT[l][:].rearrange("p (j s) -> p j s", s=nsm)
                for s in range(nseq):
                    o = (l * nseq + s) * 16
                    self.I("dve", "scalar_tensor_tensor", [self.modT[l], self.pv], [self.AM], out=self.AM[:, o:o + 8],
                           in0=mv[:, 8:16, s], scalar=1.0, in1=self.pvs(f"nmw{l}"), op0=ALU.add, op1=ALU.mult)
                    self.I("dve", "scalar_tensor_tensor", [self.modT[l], self.pv], [self.AM], out=self.AM[:, o + 8:o + 16],
                           in0=mv[:, 32:40, s], scalar=1.0, in1=self.pvs(f"nfw{l}"), op0=ALU.add, op1=ALU.mult)
            sp = P.sb([128, 4], F32, "sp")
            self.I("act", "activation", [self.pv], [sp], out=sp[:], in_=self.pvs("lam"), func=AF.Exp, scale=-1.0)
            self.I("act", "activation", [sp], [sp], out=sp[:], in_=sp[:], func=AF.Ln, bias=1.0)
            self.I("dve", "tensor_scalar", [sp], [self.lruc], out=self.lruc[:, 0:4], in0=sp[:], scalar1=-8.0, scalar2=None, op0=ALU.mult)
            self.I("dve", "tensor_scalar", [sp], [self.lruc], out=self.lruc[:, 4:8], in0=sp[:], scalar1=-16.0, scalar2=None, op0=ALU.mult)

    def modv(self, l, s, g, k):
        c = (g * 8 + k) * self.nsm + s
        return self.modT[l][:, c:c + 1]

    def amv(self, l, s, which, k):
        o = (l * self.nseq + s) * 16 + which * 8 + k
        return self.AM[:, o:o + 1]

    def trig_tables(self, posf_ap, posbuf, cosT, sinT, scr):
        ki, kf, a = scr
        TWO_PI = float(2 * np.pi)
        for tab, shift in ((sinT, float(np.pi)), (cosT, float(1.5 * np.pi))):
            self.I("dve", "tensor_scalar", [posbuf, self.cst], [a], out=a[:], in0=posf_ap, scalar1=self.C("invf"), scalar2=shift, op0=ALU.mult, op1=ALU.add)
            self.I("dve", "tensor_scalar", [a], [ki], out=ki[:], in0=a[:], scalar1=1.0 / TWO_PI, scalar2=None, op0=ALU.mult)
            self.I("dve", "tensor_copy", [ki], [kf], out=kf[:], in_=ki[:])
            self.I("dve", "scalar_tensor_tensor", [kf, a], [a], out=a[:], in0=kf[:], scalar=-TWO_PI, in1=a[:], op0=ALU.mult, op1=ALU.add)
            self.I("dve", "tensor_scalar", [a], [kf], out=kf[:], in0=a[:], scalar1=0.0, scalar2=TWO_PI, op0=ALU.is_lt, op1=ALU.mult)
            self.I("dve", "tensor_tensor", [a, kf], [a], out=a[:], in0=a[:], in1=kf[:], op=ALU.add)
            self.I("dve", "tensor_scalar", [a], [kf], out=kf[:], in0=a[:], scalar1=TWO_PI, scalar2=-TWO_PI, op0=ALU.is_ge, op1=ALU.mult)
            self.I("dve", "tensor_tensor", [a, kf], [a], out=a[:], in0=a[:], in1=kf[:], op=ALU.add)
            self.I("dve", "tensor_scalar", [a], [a], out=a[:], in0=a[:], scalar1=-float(np.pi), scalar2=-3.1415925, op0=ALU.add, op1=ALU.max)
            self.I("dve", "tensor_scalar", [a], [a], out=a[:], in0=a[:], scalar1=3.1415925, scalar2=None, op0=ALU.min)
            self.I("act", "activation", [a], [tab], out=tab[:], in_=a[:], func=AF.Sin)
        self.I("dve", "tensor_scalar", [sinT, self.cst], [sinT], out=sinT[:], in0=sinT[:], scalar1=self.C("sgn"), scalar2=None, op0=ALU.mult)

    def out_proj(self, wname, mix, l, s, tb):
        for ng in range(2):
            wb, w3 = self.wload(wname, ng)
            for mi in range(4):
                m = ng * 4 + mi
                ps = self.big()
                for c in range(8):
                    self.mm(ps, ps[:], w3[:, c, mi * 128:(mi + 1) * 128], mix[c][:], [wb, mix[c]], start=(c == 0), stop=(c == 7))
                xb = self.xs[m][tb]
                self.I("dve", "scalar_tensor_tensor", [ps, xb], [xb], out=xb[:], in0=ps[:], scalar=self.modv(l, s, 2, m), in1=xb[:], op0=ALU.mult, op1=ALU.add)

    def mixer0(self, s):
        P = self.P
        I = self.I
        with P.scope():
            nm = self.nm_alloc()
            self.alloc_psum(3, 3, 2)
            sb = P.sb
            hT = [sb([128, TB], BF16, "hT") for _ in range(8)]
            mix = [sb([128, TB], BF16, "mix") for _ in range(8)]
            xh = [sb([128, TB + 3], F32, "xh") for _ in range(4)]
            hc = [sb([128, 1], F32, "hc") for _ in range(4)]
            for c in range(4):
                I("pool", "memset", [], [xh[c]], ap=xh[c][:, 0:3], constant=0.0)
                I("pool", "memset", [], [hc[c]], ap=hc[c][:], constant=0.0)
            F = [sb([128, TB], F32, f"f{i}") for i in range(8)]
            xcb = sb([128, TB], BF16, "xcb")
            wbd = sb([128, 8, 128], BF16, "wbd")
            P.dma("pool", wbd[:], self.wbd_d.rearrange("p (c n) -> p c n", c=8), writes=[wbd])
            S32 = [[sb([128, 128], F32, "S32") for _ in range(2)] for _ in range(4)]
            Sbs = [[sb([128, 128], BF16, "Sbs") for _ in range(6)] for _ in range(4)]
            for h in range(4):
                I("pool", "memset", [], [S32[h][0]], ap=S32[h][0][:], constant=0.0)
                I("pool", "memset", [], [Sbs[h][0]], ap=Sbs[h][0][:], constant=0.0)
            posi = sb([128, TB], I32, "posi")
            posf = sb([128, TB], F32, "posf")
            cosT = sb([128, TB], F32, "cosT")
            sinT = sb([128, TB], F32, "sinT")
            ki = posi
            qb = [sb([128, TB], BF16, "qb") for _ in range(4)]
            qd = [sb([128, TB], BF16, "qd") for _ in range(4)]
            kb = [sb([128, TB], BF16, "kb") for _ in range(4)]
            sg = [sb([128, TB], BF16, "sg") for _ in range(4)]
            vtm = [sb([128, TB], BF16, "vtm") for _ in range(4)]
            ktm = [[sb([128, 128], BF16, "ktm") for _ in range(4)] for _ in range(4)]
            pt = [sb([128, 128], BF16, "pt") for _ in range(4)]
            obf = sb([128, TB], BF16, "obf")
            osq = sb([128, TB], BF16, "osq")
            c1 = lambda c: self.lruc[:, c:c + 1]
            c2 = lambda c: self.lruc[:, 4 + c:5 + c]
            G128 = [float(np.exp(LRET[h] * 128.0)) for h in range(4)]
            for tb in range(NB):
                self.norm_mod(nm, tb, lambda k: self.amv(0, s, 0, k), lambda k: self.modv(0, s, 0, k), hT)
                wxr = self.wload("w0", 0)
                wyr = self.wload("w0", 1)
                for c in range(4):
                    xc, yv, r, ig, a, sq, hh, t1 = F
                    psx = self.big()
                    self.proj_fm(wxr, hT, c, psx)
                    I("act", "activation", [psx], [xh[c]], out=xh[c][:, 3:3 + TB], in_=psx[:], func=AF.Identity)
                    psy = self.big()
                    self.proj_fm(wyr, hT, c, psy)
                    I("act", "activation", [psy], [yv], out=yv[:], in_=psy[:], func=AF.Identity)
                    I("dve", "tensor_scalar", [xh[c], self.pv], [xc], out=xc[:], in0=xh[c][:, 3:3 + TB], scalar1=self.pvs("cw3", c), scalar2=self.pvs("cb", c), op0=ALU.mult, op1=ALU.add)
                    for j in range(3):
                        I("dve", "scalar_tensor_tensor", [xh[c], xc, self.pv], [xc], out=xc[:], in0=xh[c][:, j:j + TB], scalar=self.pvs(f"cw{j}", c), in1=xc[:], op0=ALU.mult, op1=ALU.add)
                    I("dve", "tensor_copy", [xh[c]], [xh[c]], out=xh[c][:, 0:3], in_=xh[c][:, TB:TB + 3])
                    I("act", "activation", [xc], [xcb], out=xcb[:], in_=xc[:], func=AF.Identity)
                    psa = self.big()
                    self.mm(psa, psa[:], wbd[:, c, :], xcb[:], [wbd, xcb])
                    I("act", "activation", [psa, self.pv], [r], out=r[:], in_=psa[:], func=AF.Sigmoid, bias=self.pvs("ba", c))
                    psi = self.big()
                    self.mm(psi, psi[:], wbd[:, 4 + c, :], xcb[:], [wbd, xcb])
                    I("act", "activation", [psi, self.pv], [ig], out=ig[:], in_=psi[:], func=AF.Sigmoid, bias=self.pvs("bx", c))
                    I("act", "activation", [r, self.lruc], [a], out=a[:], in_=r[:], func=AF.Exp, scale=c1(c))
                    I("act", "activation", [r, self.lruc], [sq], out=sq[:], in_=r[:], func=AF.Exp, scale=c2(c))
                    I("act", "activation", [sq], [sq], out=sq[:], in_=sq[:], func=AF.Ln, scale=-1.0, bias=self.onep[:, 0:1])
                    I("act", "activation", [sq], [sq], out=sq[:], in_=sq[:], func=AF.Exp, scale=0.5)
                    I("dve", "tensor_tensor", [ig, xc], [ig], out=ig[:], in0=ig[:], in1=xc[:], op=ALU.mult)
                    I("dve", "tensor_tensor", [ig, sq], [ig], out=ig[:], in0=ig[:], in1=sq[:], op=ALU.mult)
                    I("dve", "tensor_tensor_scan", [a, ig, hc[c]], [hh], out=hh[:], data0=a[:], data1=ig[:], initial=hc[c][:, 0:1], op0=ALU.mult, op1=ALU.add)
                    I("dve", "tensor_copy", [hh], [hc[c]], out=hc[c][:, 0:1], in_=hh[:, TB - 1:TB])
                    I("pool", "tensor_tensor", [yv], [t1], out=t1[:], in0=yv[:], in1=yv[:], op=ALU.mult)
                    I("pool", "tensor_scalar", [t1], [t1], out=t1[:], in0=t1[:], scalar1=0.044715, scalar2=1.0, op0=ALU.mult, op1=ALU.add)
                    I("pool", "tensor_tensor", [t1, yv], [t1], out=t1[:], in0=t1[:], in1=yv[:], op=ALU.mult)
                    I("act", "activation", [t1], [t1], out=t1[:], in_=t1[:], func=AF.Sigmoid, scale=1.5957691216057308)
                    I("pool", "tensor_tensor", [t1, yv], [t1], out=t1[:], in0=t1[:], in1=yv[:], op=ALU.mult)
                    I("dve", "tensor_tensor", [t1, hh], [mix[c]], out=mix[c][:], in0=t1[:], in1=hh[:], op=ALU.mult)
                if CUT == 1:
                    continue
                t1, t2, o32, m2 = F[0], F[1], F[2], F[3]
                P.dma("sp", posi[:], self.pos[s:s + 1, tb * TB:(tb + 1) * TB].partition_broadcast(128), writes=[posi])
                I("dve", "tensor_copy", [posi], [posf], out=posf[:], in_=posi[:])
                self.trig_tables(posf[:], posf, cosT, sinT, (ki, F[4], F[5]))
                if CUT == 2:
                    continue
                wq = self.wload("w0", 2)
                wqs = self.wload("w0", 6)
                for h in range(4):
                    psq = self.big()
                    self.proj_fm(wq, hT, h, psq)
                    psqs = self.big()
                    self.proj_fm(wqs, hT, h, psqs)
                    I("dve", "tensor_tensor", [psq, cosT], [t1], out=t1[:], in0=psq[:], in1=cosT[:], op=ALU.mult)
                    I("dve", "tensor_tensor", [psqs, sinT], [t2], out=t2[:], in0=psqs[:], in1=sinT[:], op=ALU.mult)
                    I("dve", "tensor_tensor", [t1, t2], [t1], out=t1[:], in0=t1[:], in1=t2[:], op=ALU.add)
                    I("act", "activation", [t1], [qb[h]], out=qb[h][:], in_=t1[:], func=AF.Identity)
                    I("pool", "tensor_tensor", [t1, self.cst], [qd[h]], out=qd[h][:].rearrange("p (a b) -> p a b", a=4),
                      in0=t1[:].rearrange("p (a b) -> p a b", a=4), in1=self.C(f"qdec{h}").unsqueeze(1).to_broadcast([128, 4, 128]), op=ALU.mult)
                wk = self.wload("w0", 3)
                wks = self.wload("w0", 7)
                for h in range(4):
                    psk = self.big()
                    self.proj_fm(wk, hT, h, psk)
                    psks = self.big()
                    self.proj_fm(wks, hT, h, psks)
                    I("dve", "scalar_tensor_tensor", [psk, cosT], [t1], out=t1[:], in0=psk[:], scalar=128.0 ** -0.5, in1=cosT[:], op0=ALU.mult, op1=ALU.mult)
                    I("dve", "scalar_tensor_tensor", [psks, sinT], [t2], out=t2[:], in0=psks[:], scalar=128.0 ** -0.5, in1=sinT[:], op0=ALU.mult, op1=ALU.mult)
                    I("dve", "tensor_tensor", [t1, t2], [kb[h]], out=kb[h][:], in0=t1[:], in1=t2[:], op=ALU.add)
                wv = self.wload("w0", 4)
                for i in range(4):
                    psv = self.big()
                    self.proj_tm(wv, hT, i, psv)
                    I("act", "activation", [psv], [vtm[i]], out=vtm[i][:], in_=psv[:], func=AF.Identity)
                wg = self.wload("w0", 5)
                for h in range(4):
                    psg = self.big()
                    self.proj_fm(wg, hT, h, psg)
                    I("act", "activation", [psg], [sg[h]], out=sg[h][:], in_=psg[:], func=AF.Silu)
                if CUT == 3:
                    continue
                for h in range(4):
                    for i in range(4):
                        tp = self.tps()
                        I("pe", "transpose", [kb[h], self.cstb], [tp], out=tp[:], in_=kb[h][:, i * 128:(i + 1) * 128], identity=self.identb)
                        I("act", "activation", [tp, self.cst], [ktm[h][i]], out=ktm[h][i][:], in_=tp[:], func=AF.Identity, scale=self.C("kdec")[:, h:h + 1])
                if CUT == 4:
                    continue
                for i in range(4):
                    gi = tb * 4 + i
                    for h in range(4):
                        kvp = self.small()
                        self.mm(kvp, kvp[:], ktm[h][i][:], vtm[i][:, h * 128:(h + 1) * 128], [ktm[h][i], vtm[i]])
                        so, sn = S32[h][gi % 2], S32[h][(gi + 1) % 2]
                        bn = Sbs[h][(gi + 1) % 6]
                        I("dve", "scalar_tensor_tensor", [so, kvp], [sn], out=sn[:], in0=so[:], scalar=G128[h], in1=kvp[:], op0=ALU.mult, op1=ALU.add)
                        I("dve", "scalar_tensor_tensor", [so, kvp], [bn], out=bn[:], in0=so[:], scalar=G128[h], in1=kvp[:], op0=ALU.mult, op1=ALU.add)
                if CUT == 5:
                    continue
                for h in range(4):
                    scs = []
                    for i in range(4):
                        sc = self.small()
                        self.mm(sc, sc[:], kb[h][:, i * 128:(i + 1) * 128], qb[h][:, i * 128:(i + 1) * 128], [kb[h], qb[h]])
                        I("dve", "tensor_tensor", [sc, self.cst], [pt[i]], out=pt[i][:], in0=sc[:], in1=self.C(f"rmask{h}"), op=ALU.mult)
                    ob = self.big()
                    for i in range(4):
                        gi = tb * 4 + i
                        bo = Sbs[h][gi % 6]
                        self.mm(ob, ob[:, i * 128:(i + 1) * 128], vtm[i][:, h * 128:(h + 1) * 128], pt[i][:], [vtm[i], pt[i]], start=True, stop=False, inc=False)
                        self.mm(ob, ob[:, i * 128:(i + 1) * 128], bo[:], qd[h][:, i * 128:(i + 1) * 128], [bo, qd[h]], start=False, stop=True)
                    I("act", "activation", [ob], [o32], out=o32[:], in_=ob[:], func=AF.Identity)
                    I("pool", "tensor_copy", [o32], [obf], out=obf[:], in_=o32[:])
                    I("act", "activation", [o32], [osq], out=osq[:], in_=o32[:], func=AF.Square)
                    pm = self.big()
                    self.mm(pm, pm[:], self.onesdivb, obf[:], [obf, self.cstb])
                    pq = self.big()
                    self.mm(pq, pq[:], self.onesdivb, osq[:], [osq, self.cstb])
                    I("act", "activation", [pm], [m2], out=m2[:], in_=pm[:], func=AF.Square)
                    I("dve", "tensor_tensor", [pq, m2], [m2], out=m2[:], in0=pq[:], in1=m2[:], op=ALU.subtract)
                    I("dve", "tensor_scalar", [m2], [m2], out=m2[:], in0=m2[:], scalar1=0.0, scalar2=None, op0=ALU.max)
                    I("act", "activation", [m2], [m2], out=m2[:], in_=m2[:], func=AF.Ln, bias=self.epsc[:, 0:1])
                    I("act", "activation", [m2], [m2], out=m2[:], in_=m2[:], func=AF.Exp, scale=-0.5)
                    I("dve", "tensor_tensor", [o32, pm], [o32], out=o32[:], in0=o32[:], in1=pm[:], op=ALU.subtract)
                    I("dve", "tensor_tensor", [o32, m2], [o32], out=o32[:], in0=o32[:], in1=m2[:], op=ALU.mult)
                    I("dve", "tensor_tensor", [o32, sg[h]], [mix[4 + h]], out=mix[4 + h][:], in0=o32[:], in1=sg[h][:], op=ALU.mult)
                off = os.environ.get("M0OFF", "")
                for c in range(8):
                    if (off == "lru" and c < 4) or (off == "ret" and c >= 4):
                        I("pool", "memset", [], [mix[c]], ap=mix[c][:], constant=0.0)
                self.out_proj("wo0", mix, 0, s, tb)

    def conv_silu(self, ps, work, carry, wcol, out):
        I = self.I
        I("act", "activation", [ps], [work], out=work[:, 3:3 + TB], in_=ps[:], func=AF.Identity)
        I("dve", "tensor_copy", [carry], [work], out=work[:, 0:3], in_=carry[:])
        I("act", "activation", [ps, self.pv], [out], out=out[:], in_=ps[:], func=AF.Identity, scale=self.pvs("gcw3", wcol))
        for j in range(3):
            I("dve", "scalar_tensor_tensor", [work, out, self.pv], [out], out=out[:], in0=work[:, j:j + TB], scalar=self.pvs(f"gcw{j}", wcol), in1=out[:], op0=ALU.mult, op1=ALU.add)
        I("dve", "tensor_copy", [work], [carry], out=carry[:], in_=work[:, TB:TB + 3])
        I("act", "activation", [out], [out], out=out[:], in_=out[:], func=AF.Silu)

    def rstd_bc(self, src, sqb, dst, lhsT):
        I = self.I
        I("act", "activation", [src], [sqb], out=sqb[:], in_=src[:], func=AF.Square)
        ps = self.big()
        self.mm(ps, ps[:], lhsT, sqb[:], [sqb, self.cstb])
        I("act", "activation", [ps], [dst], out=dst[:], in_=ps[:], func=AF.Ln, bias=self.epsc[:, 0:1])
        I("act", "activation", [dst], [dst], out=dst[:], in_=dst[:], func=AF.Exp, scale=-0.5)

    def mixer1(self, s):
        P = self.P
        I = self.I
        with P.scope():
            nm = self.nm_alloc()
            self.alloc_psum(4, 2, 2)
            sb = P.sb
            hTs = [[sb([128, TB], BF16, "hT") for _ in range(8)] for _ in range(2)]
            mix = [sb([128, TB], BF16, "mix") for _ in range(8)]
            F = [sb([128, TB], F32, f"f{i}") for i in range(6)]
            sqb = sb([128, TB], BF16, "sqb")
            lbv = sb([128, 8], F32, "lbv")
            I("dve", "tensor_tensor", [self.pv], [lbv], out=lbv[:, 0:4], in0=self.pvs("lb1"), in1=self.pvs("lb0"), op=ALU.subtract)
            I("act", "activation", [lbv], [lbv], out=lbv[:, 0:4], in_=lbv[:, 0:4], func=AF.Sigmoid)
            I("dve", "tensor_scalar", [lbv], [lbv], out=lbv[:, 4:8], in0=lbv[:, 0:4], scalar1=-1.0, scalar2=1.0, op0=ALU.mult, op1=ALU.add)
            gdp = sb([128, 8], F32, "gdp")
            P.dma("sp", gdp[:], self.gdp_d.partition_broadcast(128), writes=[gdp])
            I("act", "activation", [gdp], [gdp], out=gdp[:, 0:4], in_=gdp[:, 0:4], func=AF.Exp)
            I("dve", "tensor_scalar", [gdp], [gdp], out=gdp[:, 0:4], in0=gdp[:, 0:4], scalar1=-1.0, scalar2=None, op0=ALU.mult)
            wabb = sb([128, 8, 8], BF16, "wabb")
            P.dma("pool", wabb[:], self.wab.rearrange("p (k n) -> p k n", k=8), writes=[wabb])
            rowm = sb([128, 4], F32, "rowm")
            for r in range(4):
                I("dve", "tensor_copy", [self.cst], [rowm], out=rowm[:, r:r + 1], in_=self.C("hmaskT")[:, 32 * r + 31:32 * r + 32])
            hS32 = [[sb([128, 128], F32, "hS32") for _ in range(2)] for _ in range(4)]
            hSb0 = [sb([128, 128], BF16, "hSb0") for _ in range(4)]
            for h in range(4):
                I("pool", "memset", [], [hS32[h][0]], ap=hS32[h][0][:], constant=0.0)
                I("pool", "memset", [], [hSb0[h]], ap=hSb0[h][:], constant=0.0)
            sgh = sb([128, TB], BF16, "sgh")
            gS32 = [[sb([128, 128], F32, "gS32") for _ in range(2)] for _ in range(4)]
            gSb = [[sb([128, 128], BF16, "gSb") for _ in range(2)] for _ in range(4)]
            for h in range(4):
                I("pool", "memset", [], [gS32[h][0]], ap=gS32[h][0][:], constant=0.0)
                I("pool", "memset", [], [gSb[h][0]], ap=gSb[h][0][:], constant=0.0)
            carry = [[sb([128, 3], F32, "carry") for _ in range(3)] for _ in range(4)]
            for h in range(4):
                for w_ in range(3):
                    I("pool", "memset", [], [carry[h][w_]], ap=carry[h][w_][:], constant=0.0)
            GTb = sb([128, 4, 40], F32, "GT")
            GT = GTb[:]
            gt = [P.view(GTb[:, i, :], "gtv", parent=GTb, share=True) for i in range(4)]
            shb = [sb([128, TB], BF16, f"shb{i}") for i in range(19)]
            Sring = [P.view(shb[12 + n // 4][:, (n % 4) * 128:(n % 4 + 1) * 128], "Sring", parent=shb[12 + n // 4], share=True) for n in range(17)]
            dlast = sb([128, 16], F32, "dlast")
            pt = [sb([128, 128], BF16, "pt") for _ in range(2)]
            work = sb([128, TB + 3], F32, "work")
            egT = sb([128, TB], F32, "egT")
            s32q = [sb([128, TB], F32, f"s32q{i}") for i in range(6)]
            u324 = sb([128, TB], F32, "u324")
            vnb = [sb([128, 128], BF16, "vnb") for _ in range(2)]
            hcount = [0] * 4
            gcount = [0] * 4
            nmf = lambda tb_: self.norm_mod(nm, tb_, lambda k: self.amv(1, s, 0, k), lambda k: self.modv(1, s, 0, k), hTs[tb_ % 2])
            nmf(0)
            for tb in range(NB):
                hT = hTs[tb % 2]
                qi, ki_, qbx, kl = shb[0:4]
                kt4 = shb[4:8]
                vtm = shb[8:12]
                whq = self.wload("w1i", 0)
                whf = self.wload("w1i", 1)
                whi = self.wload("w1i", 2)
                for i in range(4):
                    psv = self.big()
                    self.proj_tm(whi, hT, i, psv)
                    I("act", "activation", [psv], [vtm[i]], out=vtm[i][:], in_=psv[:], func=AF.Identity)
                whg = self.wload("w1i", 3)
                for h in range(4):
                    sgm, lf, kk, b, d1, e1 = F
                    o32 = sgm
                    psf = self.big()
                    self.proj_fm(whf, hT, h, psf)
                    I("act", "activation", [psf], [sgm], out=sgm[:], in_=psf[:], func=AF.Sigmoid)
                    psg = self.big()
                    self.proj_fm(whg, hT, h, psg)
                    I("act", "activation", [psg], [sgh], out=sgh[:], in_=psg[:], func=AF.Silu)
                    I("act", "activation", [sgm, lbv], [sgm], out=sgm[:], in_=sgm[:], func=AF.Identity, scale=lbv[:, 4 + h:5 + h], bias=lbv[:, h:h + 1])
                    I("act", "activation", [sgm], [lf], out=lf[:], in_=sgm[:], func=AF.Ln)
                    I("act", "activation", [sgm], [kk], out=kk[:], in_=sgm[:], func=AF.Identity, scale=-1.0, bias=1.0)
                    I("dve", "tensor_tensor_scan", [lf, self.cst], [b], out=b[:], data0=self.C("hrst"), data1=lf[:], initial=0.0, op0=ALU.mult, op1=ALU.add)
                    b3 = b[:].rearrange("p (n c) -> p n c", c=32)
                    v3 = lambda t: t[:].rearrange("p (n c) -> p n c", c=32)
                    I("dve", "tensor_tensor", [b], [d1], out=v3(d1), in0=b3, in1=b3[:, :, 16:17].to_broadcast([128, 16, 32]), op=ALU.subtract)
                    psq = self.big()
                    self.proj_fm(whq, hT, h, psq)
                    I("act", "activation", [d1], [e1], out=e1[:], in_=d1[:], func=AF.Exp)
                    I("dve", "tensor_tensor", [psq, e1], [qi], out=qi[:], in0=psq[:], in1=e1[:], op=ALU.mult)
                    I("act", "activation", [d1], [e1], out=e1[:], in_=d1[:], func=AF.Exp, scale=-1.0)
                    I("dve", "tensor_tensor", [kk, e1], [ki_], out=ki_[:], in0=kk[:], in1=e1[:], op=ALU.mult)
                    I("act", "activation", [b], [e1], out=e1[:], in_=b[:], func=AF.Exp)
                    I("dve", "tensor_tensor", [psq, e1], [qbx], out=qbx[:], in0=psq[:], in1=e1[:], op=ALU.mult)
                    I("dve", "tensor_tensor", [b], [d1], out=v3(d1), in0=b3[:, :, 31:32].to_broadcast([128, 16, 32]), in1=b3, op=ALU.subtract)
                    I("act", "activation", [d1], [e1], out=e1[:], in_=d1[:], func=AF.Exp)
                    I("dve", "tensor_tensor", [kk, e1], [kl], out=kl[:], in0=kk[:], in1=e1[:], op=ALU.mult)
                    I("act", "activation", [b], [dlast], out=dlast[:].unsqueeze(2), in_=b3[:, :, 31:32], func=AF.Exp)
                    tkb = self.tps().root
                    for i in range(4):
                        I("pe", "transpose", [kl, self.cstb], [tkb], inc=(i == 3), out=tkb[:, i * 128:(i + 1) * 128], in_=kl[:, i * 128:(i + 1) * 128], identity=self.identb)
                    for r in range(4):
                        if r % 2 == 0:
                            I("act", "activation", [tkb, rowm], [kt4[r]], out=kt4[r][:], in_=tkb[:, 0:512], func=AF.Identity, scale=rowm[:, r:r + 1])
                        else:
                            I("dve", "tensor_scalar", [tkb, rowm], [kt4[r]], out=kt4[r][:], in0=tkb[:, 0:512], scalar1=rowm[:, r:r + 1], scalar2=None, op0=ALU.mult)
                    I("pool", "tensor_copy", [hSb0[h]], [Sring[0]], out=Sring[0][:], in_=hSb0[h][:])
                    for n in range(16):
                        i, r = n // 4, n % 4
                        kvp = self.small()
                        self.mm(kvp, kvp[:], kt4[r][:, i * 128:(i + 1) * 128], vtm[i][:, h * 128:(h + 1) * 128], [kt4[r], vtm[i]])
                        gi = hcount[h]
                        hcount[h] += 1
                        so, sn = hS32[h][gi % 2], hS32[h][(gi + 1) % 2]
                        I("dve", "scalar_tensor_tensor", [so, kvp, dlast], [sn], out=sn[:], in0=so[:], scalar=dlast[:, n:n + 1], in1=kvp[:], op0=ALU.mult, op1=ALU.add)
                        I("dve", "scalar_tensor_tensor", [so, kvp, dlast], [Sring[n + 1]], out=Sring[n + 1][:], in0=so[:], scalar=dlast[:, n:n + 1], in1=kvp[:], op0=ALU.mult, op1=ALU.add)
                    I("pool", "tensor_copy", [Sring[16]], [hSb0[h]], out=hSb0[h][:], in_=Sring[16][:])
                    ob = self.big()
                    for i in range(4):
                        sc = self.small()
                        self.mm(sc, sc[:], ki_[:, i * 128:(i + 1) * 128], qi[:, i * 128:(i + 1) * 128], [ki_, qi])
                        p_ = pt[i % 2]
                        I("dve", "tensor_tensor", [sc, self.cst], [p_], out=p_[:], in0=sc[:], in1=self.C("hmaskT"), op=ALU.mult)
                        for r in range(4):
                            n = i * 4 + r
                            self.mm(ob, ob[:, n * 32:(n + 1) * 32], vtm[i][:, h * 128:(h + 1) * 128], p_[:, r * 32:(r + 1) * 32], [vtm[i], p_], start=True, stop=False, inc=False)
                            self.mm(ob, ob[:, n * 32:(n + 1) * 32], Sring[n][:], qbx[:, n * 32:(n + 1) * 32], [Sring[n], qbx], start=False, stop=True)
                    I("act", "activation", [ob], [o32], out=o32[:], in_=ob[:], func=AF.Identity)
                    self.rstd_bc(o32, sqb, d1, self.onesdivb)
                    I("dve", "tensor_tensor", [o32, d1], [o32], out=o32[:], in0=o32[:], in1=d1[:], op=ALU.mult)
                    I("dve", "scalar_tensor_tensor", [o32, sgh, self.pv], [mix[h]], out=mix[h][:], in0=o32[:], scalar=self.pvs("hgw", 0), in1=sgh[:], op0=ALU.mult, op1=ALU.mult)
                qT, kT, vT, qgT = shb[0:4]
                b16q = shb[4:14]
                wTnA4, wTnB4, kgA4, kgB4, qkT4 = shb[14:19]
                wTnB4 = wTnA4
                for i in range(4):
                    g_ = gt[i]
                    pab = self.small()
                    for k in range(8):
                        self.mm(pab, pab[:, 0:8], hT[k][:, i * 128:(i + 1) * 128], wabb[:, k, :], [hT[k], wabb], start=(k == 0), stop=(k == 7))
                    I("dve", "tensor_tensor", [pab, gdp], [g_], out=g_[:, 0:4], in0=pab[:, 0:4], in1=gdp[:, 4:8], op=ALU.add)
                    I("act", "activation", [pab], [g_], out=g_[:, 4:8], in_=pab[:, 4:8], func=AF.Sigmoid)
                for i in range(4):
                    g_ = gt[i]
                    I("act", "activation", [g_], [g_], out=g_[:, 0:4], in_=g_[:, 0:4], func=AF.Exp)
                    I("act", "activation", [g_], [g_], out=g_[:, 0:4], in_=g_[:, 0:4], func=AF.Ln, bias=1.0)
                    I("dve", "tensor_tensor", [g_, gdp], [g_], out=g_[:, 0:4], in0=g_[:, 0:4], in1=gdp[:, 0:4], op=ALU.mult)
                    pg = self.small()
                    for q_, nm_ in enumerate(("gU", "gsame", "gselA", "gselB")):
                        self.mm(pg, pg[:, q_ * 4:(q_ + 1) * 4], self.C(nm_), g_[:, 0:4], [g_, self.cst])
                    I("act", "activation", [pg], [g_], out=g_[:, 8:12], in_=pg[:, 0:4], func=AF.Exp)
                    I("dve", "tensor_copy", [pg], [g_], out=g_[:, 32:40], in_=pg[:, 0:8])
                    I("dve", "tensor_tensor", [g_], [g_], out=g_[:, 12:16], in0=g_[:, 36:40], in1=g_[:, 32:36], op=ALU.subtract)
                    I("act", "activation", [g_], [g_], out=g_[:, 12:16], in_=g_[:, 12:16], func=AF.Exp)
                    I("act", "activation", [pg], [g_], out=g_[:, 20:28], in_=pg[:, 8:16], func=AF.Exp)
                    I("dve", "tensor_scalar", [g_, self.cst], [g_], out=g_[:, 16:20], in0=g_[:, 12:16], scalar1=self.C("gselB")[:, 0:1], scalar2=None, op0=ALU.mult)
                    I("dve", "tensor_scalar", [g_, self.cst], [g_], out=g_[:, 12:16], in0=g_[:, 12:16], scalar1=self.C("gselA")[:, 0:1], scalar2=None, op0=ALU.mult)
                    I("dve", "tensor_tensor", [g_], [g_], out=g_[:, 28:32], in0=g_[:, 4:8], in1=g_[:, 8:12], op=ALU.mult)
                wdq = self.wload("w1i", 4)
                wdk = self.wload("w1i", 5)
                wdv = self.wload("w1i", 6)
                wdz = self.wload("w1i", 7)
                for h in range(4):
                    qc, kc, vc, rn, o32 = F[:5]
                    psq = self.big()
                    self.proj_fm(wdq, hT, h, psq)
                    self.conv_silu(psq, work, carry[h][0], h, qc)
                    psk = self.big()
                    self.proj_fm(wdk, hT, h, psk)
                    self.conv_silu(psk, work, carry[h][1], 4 + h, kc)
                    psv = self.big()
                    self.proj_fm(wdv, hT, h, psv)
                    self.conv_silu(psv, work, carry[h][2], 8 + h, vc)
                    psz = self.big()
                    self.proj_fm(wdz, hT, h, psz)
                    I("act", "activation", [psz], [sgh], out=sgh[:], in_=psz[:], func=AF.Silu)
                    I("act", "activation", [vc], [vT], out=vT[:], in_=vc[:], func=AF.Identity)
                    self.rstd_bc(kc, sqb, rn, self.onesb)
                    I("dve", "tensor_tensor", [kc, rn], [kT], out=kT[:], in0=kc[:], in1=rn[:], op=ALU.mult)
                    self.rstd_bc(qc, sqb, rn, self.onesb)
                    I("dve", "scalar_tensor_tensor", [qc, rn], [qc], out=qc[:], in0=qc[:], scalar=128.0 ** -0.5, in1=rn[:], op0=ALU.mult, op1=ALU.mult)
                    I("act", "activation", [qc], [qT], out=qT[:], in_=qc[:], func=AF.Identity)
                    TS = [slice(i * 128, (i + 1) * 128) for i in range(4)]
                    v4 = lambda b_: b_[:].rearrange("p (a b) -> p a b", a=4)
                    bc = lambda ap2: ap2.unsqueeze(1).to_broadcast([128, 4, 128])
                    sc4 = lambda col: GT[:, :, col:col + 1].to_broadcast([128, 4, 128])
                    G4, E4, ET4, Ds4, DTi4, G24 = s32q
                    A4, B4, A24, B24, IA4, P04, P14, kbg4, vb4, XT4 = b16q
                    Af4 = G4

                    def mm4(lhs, rhs, R):
                        bank = self.big()
                        for i in range(4):
                            self.mm(bank, bank[:, TS[i]], lhs(i), rhs(i), R, inc=(i == 3))
                        return bank

                    I("dve", "tensor_tensor", [self.cst, GTb], [G4], out=v4(G4), in0=bc(self.C("gMgt")), in1=sc4(h), op=ALU.mult)
                    I("pool", "tensor_tensor", [self.cst, GTb], [G24], out=v4(G24), in0=bc(self.C("ones")), in1=sc4(h), op=ALU.mult)
                    gU = self.C("gU")
                    pd4 = mm4(lambda i: gU, lambda i: G4[:, TS[i]], [G4, self.cst])
                    I("act", "activation", [pd4], [E4], out=E4[:], in_=pd4[:], func=AF.Exp)
                    pdt4 = mm4(lambda i: G4[:, TS[i]], lambda i: gU, [G4, self.cst])
                    I("act", "activation", [pdt4], [ET4], out=ET4[:], in_=pdt4[:], func=AF.Exp)
                    pgb4 = mm4(lambda i: G24[:, TS[i]], lambda i: gU, [G24, self.cst])
                    I("act", "activation", [pgb4], [egT], out=egT[:], in_=pgb4[:], func=AF.Exp)
                    I("dve", "tensor_tensor", [E4, self.cst], [Ds4], out=v4(Ds4), in0=v4(E4), in1=bc(self.C("gMstrict")), op=ALU.mult)
                    I("dve", "tensor_tensor", [Ds4, GTb], [Ds4], out=v4(Ds4), in0=v4(Ds4), in1=sc4(4 + h), op=ALU.mult)
                    I("pool", "tensor_tensor", [ET4, self.cst], [DTi4], out=v4(DTi4), in0=v4(ET4), in1=bc(self.C("gMinclT")), op=ALU.mult)
                    pkk4 = mm4(lambda i: kT[:, TS[i]], lambda i: kT[:, TS[i]], [kT])
                    I("dve", "tensor_tensor", [pkk4, Ds4], [Af4], out=Af4[:], in0=pkk4[:], in1=Ds4[:], op=ALU.mult)
                    I("pool", "tensor_copy", [Af4], [A4], out=A4[:], in_=Af4[:])
                    pqk4 = mm4(lambda i: kT[:, TS[i]], lambda i: qT[:, TS[i]], [kT, qT])
                    I("dve", "tensor_tensor", [pqk4, DTi4], [qkT4], out=qkT4[:], in0=pqk4[:], in1=DTi4[:], op=ALU.mult)
                    pbt4 = self.big()
                    for i in range(4):
                        I("pe", "transpose", [Af4, self.cst], [pbt4], inc=(i == 3), out=pbt4[:, TS[i]], in_=Af4[:, TS[i]], identity=self.identf)
                    I("act", "activation", [pbt4], [B4], out=B4[:], in_=pbt4[:], func=AF.Identity)
                    I("dve", "scalar_tensor_tensor", [pbt4, self.cst], [P04], out=v4(P04), in0=v4(pbt4), scalar=-1.0, in1=bc(self.identf), op0=ALU.mult, op1=ALU.add)
                    Ap, Bp, An, Bn, Pc, Pn = A4, B4, A24, B24, P04, P14
                    for lv in range(5):
                        pa4 = mm4(lambda i: Bp[:, TS[i]], lambda i: Ap[:, TS[i]], [Bp, Ap])
                        if lv < 4:
                            I("act", "activation", [pa4], [An], out=An[:], in_=pa4[:], func=AF.Identity)
                        I("dve", "tensor_tensor", [pa4, self.cst], [IA4], out=v4(IA4), in0=v4(pa4), in1=bc(self.identf), op=ALU.add)
                        if lv < 4:
                            pb4 = mm4(lambda i: Ap[:, TS[i]], lambda i: Bp[:, TS[i]], [Ap, Bp])
                            I("act", "activation", [pb4], [Bn], out=Bn[:], in_=pb4[:], func=AF.Identity)
                        pp4 = mm4(lambda i: IA4[:, TS[i]], lambda i: Pc[:, TS[i]], [IA4, Pc])
                        dst = XT4 if lv == 4 else Pn
                        I("dve", "tensor_copy", [pp4], [dst], out=dst[:], in_=pp4[:])
                        Ap, An = An, Ap
                        Bp, Bn = Bn, Bp
                        Pc, Pn = Pn, Pc
                    tk = self.tps().root
                    for i in range(4):
                        I("pe", "transpose", [kT, self.cstb], [tk], inc=(i == 3), out=tk[:, TS[i]], in_=kT[:, TS[i]], identity=self.identb)
                    tk4 = tk[:, 0:512].rearrange("p (a b) -> p a b", a=4)
                    I("dve", "tensor_tensor", [tk, GTb], [kbg4], out=v4(kbg4), in0=tk4, in1=sc4(28 + h), op=ALU.mult)
                    I("dve", "tensor_tensor", [tk, GTb], [kgA4], out=v4(kgA4), in0=tk4, in1=sc4(12 + h), op=ALU.mult)
                    I("dve", "tensor_tensor", [tk, GTb], [kgB4], out=v4(kgB4), in0=tk4, in1=sc4(16 + h), op=ALU.mult)
                    tv = self.tps().root
                    for i in range(4):
                        I("pe", "transpose", [vT, self.cstb], [tv], inc=(i == 3), out=tv[:, TS[i]], in_=vT[:, TS[i]], identity=self.identb)
                    I("dve", "tensor_tensor", [tv, GTb], [vb4], out=v4(vb4), in0=tv[:, 0:512].rearrange("p (a b) -> p a b", a=4), in1=sc4(4 + h), op=ALU.mult)
                    pw4 = mm4(lambda i: kbg4[:, TS[i]], lambda i: XT4[:, TS[i]], [kbg4, XT4])
                    I("dve", "tensor_scalar", [pw4], [wTnA4], out=wTnA4[:], in0=pw4[:], scalar1=-1.0, scalar2=None, op0=ALU.mult)
                    pu4 = mm4(lambda i: XT4[:, TS[i]], lambda i: vb4[:, TS[i]], [XT4, vb4])
                    I("act", "activation", [pu4], [u324], out=u324[:], in_=pu4[:], func=AF.Identity)
                    I("dve", "tensor_tensor", [qc, egT], [qgT], out=qgT[:], in0=qc[:], in1=egT[:], op=ALU.mult)
                    ob = self.big()
                    for n in range(8):
                        i, half = n // 2, n % 2
                        g_ = gt[i]
                        gi = gcount[h]
                        gcount[h] += 1
                        so, sn = gS32[h][gi % 2], gS32[h][(gi + 1) % 2]
                        bo, bn = gSb[h][gi % 2], gSb[h][(gi + 1) % 2]
                        wT4 = wTnA4 if half == 0 else wTnB4
                        kg4 = kgA4 if half == 0 else kgB4
                        ti = slice(i * 128, (i + 1) * 128)
                        lastc = 20 + 4 * half + h
                        pv_ = self.small()
                        self.mm(pv_, pv_[:], wT4[:, ti], bo[:], [wT4, bo])
                        vn = vnb[n % 2]
                        I("dve", "tensor_tensor", [pv_, u324], [vn], out=vn[:], in0=pv_[:], in1=u324[:, ti], op=ALU.add)
                        cs = slice(n * 64, (n + 1) * 64)
                        self.mm(ob, ob[:, cs], bo[:], qgT[:, cs], [bo, qgT], start=True, stop=False, inc=False)
                        self.mm(ob, ob[:, cs], vn[:], qkT4[:, i * 128 + half * 64:i * 128 + (half + 1) * 64], [vn, qkT4], start=False, stop=True)
                        pkv = self.small()
                        self.mm(pkv, pkv[:], kg4[:, ti], vn[:], [kg4, vn])
                        I("dve", "scalar_tensor_tensor", [so, pkv, g_], [bn], out=bn[:], in0=so[:], scalar=g_[:, lastc:lastc + 1], in1=pkv[:], op0=ALU.mult, op1=ALU.add)
                        I("dve", "scalar_tensor_tensor", [so, pkv, g_], [sn], out=sn[:], in0=so[:], scalar=g_[:, lastc:lastc + 1], in1=pkv[:], op0=ALU.mult, op1=ALU.add)
                    I("act", "activation", [ob], [o32], out=o32[:], in_=ob[:], func=AF.Identity)
                    self.rstd_bc(o32, sqb, rn, self.onesdivb)
                    I("dve", "tensor_tensor", [o32, rn], [o32], out=o32[:], in0=o32[:], in1=rn[:], op=ALU.mult)
                    I("dve", "scalar_tensor_tensor", [o32, sgh, self.pv], [mix[4 + h]], out=mix[4 + h][:], in0=o32[:], scalar=self.pvs("gdw", 0), in1=sgh[:], op0=ALU.mult, op1=ALU.mult)
                off = os.environ.get("M1OFF", "")
                for c in range(8):
                    if (off == "hg" and c < 4) or (off == "gd" and c >= 4):
                        I("pool", "memset", [], [mix[c]], ap=mix[c][:], constant=0.0)
                if tb + 1 < NB:
                    nmf(tb + 1)
                self.out_proj("wo1", mix, 1, s, tb)

    def mlp(self, l, s):
        P = self.P
        with P.scope():
            nm = self.nm_alloc()
            nm["ps"] = P.ps([128, TB], F32, "nps")
            hTs = [[P.sb([128, TB], BF16, "hT") for _ in range(8)] for _ in range(2)]
            hid = [P.sb([128, TB], BF16, "hid") for _ in range(32)]
            r32 = [P.sb([128, TB], F32, "r32") for _ in range(2)]
            pA = [P.ps([128, TB], F32, "pA") for _ in range(2)]
            pB = [P.ps([128, TB], F32, "pB") for _ in range(4)]
            if l == 0 and s == 0:
                self.late_conv()
            nmf = lambda tb_: self.norm_mod(nm, tb_, lambda k: self.amv(l, s, 1, k), lambda k: self.modv(l, s, 3, k), hTs[tb_ % 2])
            nmf(0)
            for tb in range(NB):
                hT = hTs[tb % 2]
                for g in range(8):
                    w = self.wload(f"m1_{l}", g)
                    for j in range(4):
                        c = g * 4 + j
                        ps = pA[c % 2]
                        self.proj_fm(w, hT, j, ps)
                        r = r32[c % 2]
                        self.I("act", "activation", [ps], [r], out=r[:], in_=ps[:], func=AF.Relu)
                        self.I("dve", "tensor_tensor", [r], [hid[c]], out=hid[c][:], in0=r[:], in1=r[:], op=ALU.mult)
                if tb + 1 < NB:
                    nmf(tb + 1)
                for ng in range(2):
                    for kg in range(4):
                        wb, w3 = self.wload(f"m2_{l}", kg * 2 + ng)
                        for mi in range(4):
                            for c in range(8):
                                self.mm(pB[mi], pB[mi][:], w3[:, c, mi * 128:(mi + 1) * 128], hid[kg * 8 + c][:], [wb, hid[kg * 8 + c]],
                                        start=(kg == 0 and c == 0), stop=(kg == 3 and c == 7), inc=(c == 7))
                    for mi in range(4):
                        m = ng * 4 + mi
                        xb = self.xs[m][tb]
                        self.I("dve", "scalar_tensor_tensor", [pB[mi], xb], [xb], out=xb[:], in0=pB[mi][:],
                               scalar=self.modv(l, s, 5, m), in1=xb[:], op0=ALU.mult, op1=ALU.add)

    def final(self, s):
        P = self.P
        with P.scope():
            nm = self.nm_alloc()
            nm["ps"] = P.ps([128, TB], F32, "nps")
            ob = [P.sb([128, TB], F32, "ob") for _ in range(4)]
            for tb in range(NB):
                rs = self.norm_stats(nm, tb)
                for k in range(8):
                    o = ob[k % 4]
                    self.I("dve", "scalar_tensor_tensor", [self.xs[k][tb], rs], [o], out=o[:], in0=self.xs[k][tb][:],
                           scalar=self.pvs("fnw", k), in1=rs[:], op0=ALU.mult, op1=ALU.mult)
                    P.dma("sp", self.outT[s, k * 128:(k + 1) * 128, tb * TB:(tb + 1) * TB], o[:], reads=[o])

    def load_x(self, s):
        for k in range(8):
            self.P.dma("sp", self.xs_t[k][:], self.xT[s, k * 128:(k + 1) * 128, :], writes=self.xs[k])

    def build(self, stage=9):
        P = self.P
        with P.stack:
            self.setup()
            for s in range(self.nseq):
                if stage < 1:
                    break
                self.load_x(s)
                for l in range(self.nlayers):
                    if not self.skip_mixer:
                        (self.mixer0 if l == 0 else self.mixer1)(s)
                    if stage >= 2:
                        self.mlp(l, s)
                self.final(s)
            P.barrier(final=True)
            P.emit()
        return self.nc


def _prep_shared(inp):
    f = lambda a: np.ascontiguousarray(np.asarray(a, np.float32))
    w_in = f(inp["ev_w_in"][0])
    perm = np.concatenate([np.arange(h * 128 + 64, h * 128 + 128).tolist() + np.arange(h * 128, h * 128 + 64).tolist() for h in range(4)]).astype(np.int64)
    w0 = np.concatenate([w_in, w_in[:, 1024 + perm], w_in[:, 1536 + perm]], axis=1)
    od = f(inp["od_w_in"][0])
    wab = np.ascontiguousarray(od[:, 4096:4104].reshape(8, 128, 8).transpose(1, 0, 2).reshape(128, 64))
    wa = f(inp["lru_w_a"][0])
    wx = f(inp["lru_w_x"][0])
    wbd = np.zeros((128, 8, 128), np.float32)
    for c in range(4):
        for half in range(2):
            sl = slice(half * 64, half * 64 + 64)
            wbd[sl, c, sl] = wa[2 * c + half]
            wbd[sl, 4 + c, sl] = wx[2 * c + half]
    pk = _params(inp)
    shared = {
        "ada_w": f(inp["ada_w"]), "ada_b": f(inp["ada_b"]),
        "w0": np.ascontiguousarray(w0), "wo0": f(inp["ev_w_out"][0]),
        "w1i": np.ascontiguousarray(od[:, :4096]), "wo1": f(inp["od_w_out"][0]),
        "m1_0": f(inp["mlp_w1"][0]), "m2_0": f(inp["mlp_w2"][0]),
        "m1_1": f(inp["mlp_w1"][1]), "m2_1": f(inp["mlp_w2"][1]),
        "wab": wab, "pv": pk.array(), "cst": _CST.array(), "wbd": np.ascontiguousarray(wbd.reshape(128, 1024)),
        "gdp": np.ascontiguousarray(np.concatenate([f(inp["gd_a_log"][0]), f(inp["gd_dt_bias"][0])])[None, :]),
    }
    return shared, pk


def _core_inputs(inp, shared, seqs):
    x = np.asarray(inp["x"], np.float32)
    c = np.asarray(inp["c"], np.float32)
    pos = np.asarray(inp["positions"], np.int32)
    ns = len(seqs)
    m = dict(shared)
    m["xT"] = np.ascontiguousarray(np.stack([x[b].T for b in seqs]))
    cs = list(seqs) if len(seqs) > 1 else [seqs[0], seqs[0]]
    cc = np.stack([c[b] for b in cs], 1)
    ns = len(cs)
    m["cT"] = np.ascontiguousarray(cc.reshape(8, 128, ns).transpose(1, 0, 2).reshape(128, 8 * ns))
    m["pos"] = np.ascontiguousarray(np.stack([pos[b] for b in seqs]))
    return m


_NC_CACHE = {}


def kernel(**inputs):
    shared, pk = _prep_shared(inputs)
    ncores, nseq = 8, 2
    key = (nseq, 2)
    if key not in _NC_CACHE:
        _NC_CACHE[key] = Builder(pk.off, pk.n, nseq=nseq, nlayers=2).build()
    nc = _NC_CACHE[key]
    in_maps = [_core_inputs(inputs, shared, [i * nseq + j for j in range(nseq)]) for i in range(ncores)]
    res = run_bass_kernel_spmd(nc, in_maps, core_ids=list(range(ncores)))
    out = np.empty((16, T, 1024), np.float32)
    for i in range(ncores):
        o = res.results[i]["outT"]
        for j in range(nseq):
            out[i * nseq + j] = o[j].T
    return out
```
